# Optimizing a Trainium2 kernel written in Bass

```python
import jax, jax.numpy as jnp
from jax import lax
import numpy as np

D_MODEL = 1024
BATCH = 8
SEQ = 4096
DEPTH = 4

GRID_W = 64
CTX_LEN = 256
N_MIXERS = 3
N_LAYERS_A = (DEPTH + 2) // 3
N_LAYERS_B = (DEPTH + 1) // 3
N_LAYERS_C = DEPTH // 3
N_SUB = 3
N_MOD = 3 * N_SUB
D_FF = 256 * ((8 * D_MODEL // 3 + 255) // 256)
FFN_RES = 0.5
RET_DK = 256
RET_HEADS = D_MODEL // RET_DK
RET_DV = 2 * RET_DK
RET_CHUNK = 128
ROPE_BASE = 10000.0
NA_HEAD_DIM = 64
NA_HEADS = D_MODEL // NA_HEAD_DIM
NA_WIN_R = 8
NA_WIN_C = 16
NEG_INF = -1e30
LRU_WIDTH = D_MODEL
LRU_BLOCK_W = 256
LRU_BLOCKS = LRU_WIDTH // LRU_BLOCK_W
LRU_CONV_W = 4
LRU_C = 8.0
DEEPNORM_ALPHA = (2 * DEPTH) ** 0.25
DEEPNORM_BETA = (8 * DEPTH) ** -0.25
LN_EPS = 1e-5

kernel_name = 'hybrid_retention_natten_rglru_dit'


def layer_norm(x, g, b):
    xf = x.astype(jnp.float32)
    mu = jnp.mean(xf, axis=-1, keepdims=True)
    var = jnp.mean(jnp.square(xf - mu), axis=-1, keepdims=True)
    y = (xf - mu) * lax.rsqrt(var + LN_EPS) * g.astype(jnp.float32) + b.astype(jnp.float32)
    return y.astype(x.dtype)


def modulate(h, shift, scale):
    return h * (1.0 + scale) + shift


def post_norm(h, delta, g, b):
    return layer_norm(DEEPNORM_ALPHA * h + delta, g, b)


def swiglu(h, w_in, w_out):
    gate, up = jnp.split(h @ w_in, 2, axis=-1)
    return (jax.nn.silu(gate) * up) @ w_out


def half_ffn(h, shift, scale, gate, w_in, w_out, g, b):
    return post_norm(h, gate * (FFN_RES * swiglu(modulate(h, shift, scale), w_in, w_out)), g, b)


def retention_log_gammas(reverse):
    h = jnp.arange(RET_HEADS, dtype=jnp.float32)
    if reverse:
        h = h[::-1]
    return jnp.log1p(-jnp.exp2(-5.0 - h))


def axial_rope(t):
    n, dk = t.shape[2], t.shape[3]
    half = dk // 2
    pos = jnp.arange(n)
    freqs = ROPE_BASE ** (-jnp.arange(0, half, 2, dtype=jnp.float32) / half)

    def rot(u, p):
        ang = p.astype(jnp.float32)[:, None] * freqs
        cos, sin = jnp.cos(ang), jnp.sin(ang)
        u1, u2 = jnp.split(u, 2, axis=-1)
        return jnp.concatenate([u1 * cos - u2 * sin, u1 * sin + u2 * cos], axis=-1)

    return jnp.concatenate([rot(t[..., :half], pos // GRID_W), rot(t[..., half:], pos % GRID_W)], axis=-1)


def retention_scan(q, k, v, log_g, s0, inclusive):
    b, h, n, _ = q.shape
    nc = n // RET_CHUNK
    pos = jnp.arange(RET_CHUNK, dtype=jnp.float32)
    diff = pos[:, None] - pos[None, :]
    visible = diff >= 0 if inclusive else diff > 0
    intra = jnp.where(visible, jnp.exp(log_g[:, None, None] * jnp.maximum(diff, 0.0)), 0.0)
    q_dec = jnp.exp(log_g[:, None] * (pos + 1.0))[..., None]
    k_dec = jnp.exp(log_g[:, None] * (RET_CHUNK - 1.0 - pos))[..., None]
    chunk_dec = jnp.exp(log_g * RET_CHUNK)[:, None, None]

    def to_chunks(t):
        return jnp.moveaxis(t.reshape(b, h, nc, RET_CHUNK, t.shape[-1]), 2, 0)

    def step(state, blk):
        qb, kb, vb = blk
        att = jnp.einsum('bhid,bhjd->bhij', qb, kb) * intra
        o = jnp.einsum('bhij,bhjv->bhiv', att, vb) + jnp.einsum('bhid,bhdv->bhiv', qb * q_dec, state)
        state = state * chunk_dec + jnp.einsum('bhjd,bhjv->bhdv', kb * k_dec, vb)
        return state, o

    s_final, o = lax.scan(step, s0, (to_chunks(q), to_chunks(k), to_chunks(v)))
    o = jnp.moveaxis(o, 0, 2).reshape(b, h, n, v.shape[-1])
    return o, s_final


def retention_bidir(q, k, v, s0_f, s0_b):
    o_f, s_f = retention_scan(q, k, v, retention_log_gammas(False), s0_f, True)
    flip = lambda t: jnp.flip(t, axis=2)
    o_b, s_b = retention_scan(flip(q), flip(k), flip(v), retention_log_gammas(True), s0_b, False)
    return o_f + flip(o_b), s_f, s_b


def retention_mixer(h_lat, h_ctx, w_in, w_out, with_ctx):
    d_qk = RET_HEADS * RET_DK
    d_v = RET_HEADS * RET_DV

    def project(h, rope):
        b, n, _ = h.shape
        q, k, v, g = jnp.split(h @ w_in, [d_qk, 2 * d_qk, 2 * d_qk + d_v], axis=-1)
        heads = lambda t, d: t.reshape(b, n, RET_HEADS, d).transpose(0, 2, 1, 3).astype(jnp.float32)
        q = heads(q, RET_DK)
        k = heads(k, RET_DK) * (RET_DK ** -0.5)
        v = heads(v, RET_DV)
        if rope:
            q, k = axial_rope(q), axial_rope(k)
        return q, k, v, g

    def finish(o, g):
        mu = jnp.mean(o, axis=-1, keepdims=True)
        var = jnp.mean(jnp.square(o - mu), axis=-1, keepdims=True)
        o = (o - mu) * lax.rsqrt(var + LN_EPS)
        b, h, n, dv = o.shape
        o = o.transpose(0, 2, 1, 3).reshape(b, n, h * dv).astype(g.dtype)
        return (jax.nn.silu(g) * o) @ w_out

    qc, kc, vc, gc = project(h_ctx, False)
    zeros = jnp.zeros((h_ctx.shape[0], RET_HEADS, RET_DK, RET_DV), jnp.float32)
    o_ctx, s_f, s_b = retention_bidir(qc, kc, vc, zeros, zeros)
    ql, kl, vl, gl = project(h_lat, True)
    o_lat, _, _ = retention_bidir(ql, kl, vl, s_f, s_b)
    y_lat = finish(o_lat, gl)
    y_ctx = finish(o_ctx, gc) if with_ctx else None
    return y_lat, y_ctx


def neighborhood_mixer(h_lat, h_ctx, w_qkv, rpb, w_out, with_ctx):
    b, n, _ = h_lat.shape
    n_ctx = h_ctx.shape[1]
    rows = n // GRID_W
    kr = min(NA_WIN_R, rows)
    scale = NA_HEAD_DIM ** -0.5
    u = (h_lat @ w_qkv).reshape(b, rows, GRID_W, 3, NA_HEADS, NA_HEAD_DIM)
    q, k, v = u[:, :, :, 0], u[:, :, :, 1], u[:, :, :, 2]
    uc = (h_ctx @ w_qkv).reshape(b, n_ctx, 3, NA_HEADS, NA_HEAD_DIM)
    qc, kc, vc = uc[:, :, 0], uc[:, :, 1], uc[:, :, 2]

    col = jnp.arange(GRID_W)
    col_start = jnp.clip(col - NA_WIN_C // 2, 0, GRID_W - NA_WIN_C)
    col_ok = (col[None, :] >= col_start[:, None]) & (col[None, :] < col_start[:, None] + NA_WIN_C)
    rel_c = jnp.clip(col[None, :] - col[:, None] + NA_WIN_C - 1, 0, 2 * NA_WIN_C - 2)

    def row_block(r):
        rs = jnp.clip(r - kr // 2, 0, rows - kr)
        q_r = lax.dynamic_index_in_dim(q, r, axis=1, keepdims=False)
        k_r = lax.dynamic_slice_in_dim(k, rs, kr, axis=1)
        v_r = lax.dynamic_slice_in_dim(v, rs, kr, axis=1)
        rel_r = rs + jnp.arange(kr) - r + NA_WIN_R - 1
        bias = rpb[:, rel_r[None, :, None], rel_c[:, None, :]].astype(jnp.float32)
        bias = jnp.where(col_ok[None, :, None, :], bias, NEG_INF)
        s_loc = jnp.einsum('bqhd,bkwhd->bhqkw', q_r, k_r).astype(jnp.float32) * scale + bias
        s_loc = s_loc.reshape(b, NA_HEADS, GRID_W, kr * GRID_W)
        s_ctx = jnp.einsum('bqhd,blhd->bhql', q_r, kc).astype(jnp.float32) * scale
        p = jax.nn.softmax(jnp.concatenate([s_loc, s_ctx], axis=-1), axis=-1).astype(v.dtype)
        p_loc = p[..., :kr * GRID_W].reshape(b, NA_HEADS, GRID_W, kr, GRID_W)
        p_ctx = p[..., kr * GRID_W:]
        return jnp.einsum('bhqkw,bkwhd->bqhd', p_loc, v_r) + jnp.einsum('bhql,blhd->bqhd', p_ctx, vc)

    o = lax.map(row_block, jnp.arange(rows))
    y_lat = jnp.moveaxis(o, 0, 1).reshape(b, n, NA_HEADS * NA_HEAD_DIM) @ w_out
    y_ctx = None
    if with_ctx:
        s = jnp.einsum('bqhd,bkhd->bhqk', qc, kc).astype(jnp.float32) * scale
        p = jax.nn.softmax(s, axis=-1).astype(vc.dtype)
        oc = jnp.einsum('bhqk,bkhd->bqhd', p, vc).reshape(b, n_ctx, NA_HEADS * NA_HEAD_DIM)
        y_ctx = oc @ w_out
    return y_lat, y_ctx


def centred_depthwise_conv(x, w, bias):
    left = LRU_CONV_W // 2
    y = lax.conv_general_dilated(x, w[:, None, :], window_strides=(1,),
                                 padding=[(left, LRU_CONV_W - 1 - left)],
                                 dimension_numbers=('NWC', 'WIO', 'NWC'),
                                 feature_group_count=x.shape[-1])
    return y + bias


def linear_scan(a, u, h0):
    def combine(e1, e2):
        a1, b1 = e1
        a2, b2 = e2
        return a1 * a2, a2 * b1 + b2
    a_cum, h = lax.associative_scan(combine, (a, u), axis=1)
    return a_cum * h0[:, None, :] + h


def rglru_gates(x, w_a, b_a, w_x, b_x, lam):
    b, n, _ = x.shape
    xb = x.reshape(b, n, LRU_BLOCKS, LRU_BLOCK_W)
    r = jax.nn.sigmoid(jnp.einsum('bnki,kij->bnkj', xb, w_a.astype(jnp.float32)).reshape(b, n, LRU_WIDTH) + b_a.astype(jnp.float32))
    i = jax.nn.sigmoid(jnp.einsum('bnki,kij->bnkj', xb, w_x.astype(jnp.float32)).reshape(b, n, LRU_WIDTH) + b_x.astype(jnp.float32))
    log_a = -LRU_C * r * jax.nn.softplus(-lam.astype(jnp.float32))
    a = jnp.exp(log_a)
    return a, jnp.sqrt(-jnp.expm1(2.0 * log_a)) * (i * x)


def rglru_mixer(h_lat, h_ctx, w_in, conv_w, conv_b, w_a, b_a, w_x, b_x, lam, w_out, with_ctx):
    def branches(h):
        gate, xr = jnp.split(h @ w_in, 2, axis=-1)
        return gate, centred_depthwise_conv(xr, conv_w, conv_b).astype(jnp.float32)

    gate_ctx, x_ctx = branches(h_ctx)
    gate_lat, x_lat = branches(h_lat)
    h0 = jnp.zeros((h_ctx.shape[0], LRU_WIDTH), jnp.float32)
    outs_ctx, outs_lat = [], []
    for d in range(2):
        orient = (lambda t: jnp.flip(t, axis=1)) if d == 1 else (lambda t: t)
        a_c, u_c = rglru_gates(orient(x_ctx), w_a[d], b_a[d], w_x[d], b_x[d], lam[d])
        hc = linear_scan(a_c, u_c, h0)
        a_l, u_l = rglru_gates(orient(x_lat), w_a[d], b_a[d], w_x[d], b_x[d], lam[d])
        hl = linear_scan(a_l, u_l, hc[:, -1])
        outs_lat.append(orient(hl))
        if with_ctx:
            outs_ctx.append(orient(hc))
    y_lat = (jax.nn.gelu(gate_lat) * (outs_lat[0] + outs_lat[1]).astype(gate_lat.dtype)) @ w_out
    y_ctx = None
    if with_ctx:
        y_ctx = (jax.nn.gelu(gate_ctx) * (outs_ctx[0] + outs_ctx[1]).astype(gate_ctx.dtype)) @ w_out
    return y_lat, y_ctx


def setup_inputs(seed: int = 0) -> dict:
    key = jax.random.key(seed)
    ks = list(jax.random.split(key, 32))
    nrm = lambda shape, s: jax.random.normal(ks.pop(), shape, jnp.float32) * s
    D = D_MODEL
    x = nrm((BATCH, SEQ, D), 1.0)
    c = nrm((BATCH, D), 1.0)
    ctx = nrm((BATCH, CTX_LEN, D), 1.0)
    c_ctx = nrm((D,), 1.0)
    ada_w = nrm((DEPTH, D, N_MOD * D), D ** -0.5)
    ada_b = nrm((DEPTH, N_MOD * D), 0.02)
    ln_g = 1.0 + nrm((DEPTH, N_SUB, D), 0.02)
    ln_b = nrm((DEPTH, N_SUB, D), 0.02)
    ffn_w_in = nrm((DEPTH, 2, D, 2 * D_FF), D ** -0.5)
    ffn_w_out = nrm((DEPTH, 2, D_FF, D), D_FF ** -0.5 * DEEPNORM_BETA)
    ret_w_in = nrm((N_LAYERS_A, D, RET_HEADS * (2 * RET_DK + 2 * RET_DV)), D ** -0.5)
    ret_w_out = nrm((N_LAYERS_A, RET_HEADS * RET_DV, D), (RET_HEADS * RET_DV) ** -0.5 * DEEPNORM_BETA)
    na_w_qkv = nrm((N_LAYERS_B, D, 3 * NA_HEADS * NA_HEAD_DIM), D ** -0.5)
    na_rpb = nrm((N_LAYERS_B, NA_HEADS, 2 * NA_WIN_R - 1, 2 * NA_WIN_C - 1), 0.1)
    na_w_out = nrm((N_LAYERS_B, NA_HEADS * NA_HEAD_DIM, D), (NA_HEADS * NA_HEAD_DIM) ** -0.5 * DEEPNORM_BETA)
    lru_w_in = nrm((N_LAYERS_C, D, 2 * LRU_WIDTH), D ** -0.5)
    lru_conv_w = nrm((N_LAYERS_C, LRU_CONV_W, LRU_WIDTH), LRU_CONV_W ** -0.5)
    lru_conv_b = nrm((N_LAYERS_C, LRU_WIDTH), 0.02)
    lru_w_a = nrm((N_LAYERS_C, 2, LRU_BLOCKS, LRU_BLOCK_W, LRU_BLOCK_W), LRU_BLOCK_W ** -0.5)
    lru_b_a = nrm((N_LAYERS_C, 2, LRU_WIDTH), 0.02)
    lru_w_x = nrm((N_LAYERS_C, 2, LRU_BLOCKS, LRU_BLOCK_W, LRU_BLOCK_W), LRU_BLOCK_W ** -0.5)
    lru_b_x = nrm((N_LAYERS_C, 2, LRU_WIDTH), 0.02)
    u = jax.random.uniform(ks.pop(), (N_LAYERS_C, 2, LRU_WIDTH), jnp.float32, minval=0.9, maxval=0.999)
    a = u ** (1.0 / LRU_C)
    lru_lam = jnp.log(a) - jnp.log1p(-a)
    lru_w_out = nrm((N_LAYERS_C, LRU_WIDTH, D), LRU_WIDTH ** -0.5 * DEEPNORM_BETA)
    return {'x': x, 'c': c, 'ctx': ctx, 'c_ctx': c_ctx, 'ada_w': ada_w, 'ada_b': ada_b,
            'ln_g': ln_g, 'ln_b': ln_b, 'ffn_w_in': ffn_w_in, 'ffn_w_out': ffn_w_out,
            'ret_w_in': ret_w_in, 'ret_w_out': ret_w_out,
            'na_w_qkv': na_w_qkv, 'na_rpb': na_rpb, 'na_w_out': na_w_out,
            'lru_w_in': lru_w_in, 'lru_conv_w': lru_conv_w, 'lru_conv_b': lru_conv_b,
            'lru_w_a': lru_w_a, 'lru_b_a': lru_b_a, 'lru_w_x': lru_w_x, 'lru_b_x': lru_b_x,
            'lru_lam': lru_lam, 'lru_w_out': lru_w_out}


def reference(x, c, ctx, c_ctx, ada_w, ada_b, ln_g, ln_b, ffn_w_in, ffn_w_out,
              ret_w_in, ret_w_out, na_w_qkv, na_rpb, na_w_out,
              lru_w_in, lru_conv_w, lru_conv_b, lru_w_a, lru_b_a, lru_w_x, lru_b_x,
              lru_lam, lru_w_out):
    h_lat, h_ctx = x, ctx
    s_lat, s_ctx = jax.nn.silu(c), jax.nn.silu(c_ctx)
    for layer in range(DEPTH):
        last = layer == DEPTH - 1
        m_l = (s_lat @ ada_w[layer] + ada_b[layer]).reshape(-1, N_MOD, 1, D_MODEL)
        m_l = [m_l[:, j] for j in range(N_MOD)]
        m_c = (s_ctx @ ada_w[layer] + ada_b[layer]).reshape(N_MOD, D_MODEL)
        m_c = [m_c[j] for j in range(N_MOD)]
        g, bb = ln_g[layer], ln_b[layer]
        h_lat = half_ffn(h_lat, m_l[0], m_l[1], m_l[2], ffn_w_in[layer, 0], ffn_w_out[layer, 0], g[0], bb[0])
        h_ctx = half_ffn(h_ctx, m_c[0], m_c[1], m_c[2], ffn_w_in[layer, 0], ffn_w_out[layer, 0], g[0], bb[0])
        u_lat = modulate(h_lat, m_l[3], m_l[4])
        u_ctx = modulate(h_ctx, m_c[3], m_c[4])
        kind, idx = layer % N_MIXERS, layer // N_MIXERS
        if kind == 0:
            y_lat, y_ctx = retention_mixer(u_lat, u_ctx, ret_w_in[idx], ret_w_out[idx], not last)
        elif kind == 1:
            y_lat, y_ctx = neighborhood_mixer(u_lat, u_ctx, na_w_qkv[idx], na_rpb[idx], na_w_out[idx], not last)
        else:
            y_lat, y_ctx = rglru_mixer(u_lat, u_ctx, lru_w_in[idx], lru_conv_w[idx], lru_conv_b[idx],
                                       lru_w_a[idx], lru_b_a[idx], lru_w_x[idx], lru_b_x[idx],
                                       lru_lam[idx], lru_w_out[idx], not last)
        h_lat = post_norm(h_lat, m_l[5] * y_lat, g[1], bb[1])
        if not last:
            h_ctx = post_norm(h_ctx, m_c[5] * y_ctx, g[1], bb[1])
        h_lat = half_ffn(h_lat, m_l[6], m_l[7], m_l[8], ffn_w_in[layer, 1], ffn_w_out[layer, 1], g[2], bb[2])
        if not last:
            h_ctx = half_ffn(h_ctx, m_c[6], m_c[7], m_c[8], ffn_w_in[layer, 1], ffn_w_out[layer, 1], g[2], bb[2])
    return h_lat
```

```python
import math
from contextlib import ExitStack

import numpy as np
import concourse.bass as bass
import concourse.mybir as mybir
from concourse.bass_utils import run_bass_kernel_spmd

F32 = mybir.dt.float32
BF16 = mybir.dt.bfloat16
AF = mybir.ActivationFunctionType
ALU = mybir.AluOpType

D = 1024
NLAT = 4096
NCTX = 256
NTOK = NLAT + NCTX
NT = NTOK // 128
DFF = 2816
DEPTH = 4
ALPHA = (2 * DEPTH) ** 0.25
EPS = 1e-5
N_CORES = 8


class Op:
    __slots__ = ("eng", "fn", "reads", "writes", "slot", "deps", "ticket")

    def __init__(self, eng, fn, reads, writes, slot):
        self.eng = eng
        self.fn = fn
        self.reads = reads
        self.writes = writes
        self.slot = slot
        self.deps = ()
        self.ticket = None


class Sched:
    ENGS = ("pe", "act", "dve", "pool", "sp")

    def __init__(self, nc, stack, nsem=100):
        self.nc = nc
        self.sems = [stack.enter_context(nc.semaphore(f"s{i}")) for i in range(nsem)]
        self.eng_sem = {e: i for i, e in enumerate(("pe", "act", "dve", "pool"))}
        self.bar_sem = 4
        self.bar_count = 0
        self.first_slot_sem = 5
        self.cur = []
        self.nops = 0
        self.counts = {e: 0 for e in self.ENGS}
        self.slot_sem = {}
        self.slot_cnt = {}
        self.next_sem = nsem - 1

    def add(self, eng, fn, reads=(), writes=(), slot=None):
        self.cur.append(Op(eng, fn, tuple(reads), tuple(writes), slot))

    def dma(self, eng, out, in_, reads, writes, slot):
        self.add(eng, lambda e: e.dma_start(out=out, in_=in_), reads, writes, slot)

    def flush(self):
        ops = self.cur
        self.cur = []
        n = len(ops)
        if n == 0:
            return
        self.nops += n
        last_writer = {}
        readers = {}
        has_dep = [False] * n
        for i, op in enumerate(ops):
            deps = set()
            for k in op.reads:
                w = last_writer.get(k)
                if w is not None:
                    deps.add(w)
            for k in op.writes:
                w = last_writer.get(k)
                if w is not None:
                    deps.add(w)
                rd = readers.get(k)
                if rd:
                    deps.update(rd[0].values())
                    deps.update(rd[1])
            deps.discard(i)
            if op.eng == "pe" and op.slot is None:
                deps = {d for d in deps if not (ops[d].eng == "pe" and ops[d].slot is None)}
            op.deps = deps
            for d in deps:
                has_dep[d] = True
            for k in op.reads:
                rd = readers.get(k)
                if rd is None:
                    rd = readers[k] = ({}, [])
                if op.slot is None:
                    rd[0][op.eng] = i
                else:
                    rd[1].append(i)
            for k in op.writes:
                last_writer[k] = i
                readers[k] = ({}, [])
        last_compute = {}
        for i, op in enumerate(ops):
            if op.slot is None:
                last_compute[op.eng] = i
        for e, i in last_compute.items():
            has_dep[i] = True
        counts = self.counts
        slots_used = {}
        local_slot = {}
        for i, op in enumerate(ops):
            if op.slot is not None:
                if op.slot not in local_slot:
                    sem = self.first_slot_sem + len(local_slot)
                    assert sem < len(self.sems), "out of DMA slot semaphores"
                    local_slot[op.slot] = sem
                sem = local_slot[op.slot]
                self.slot_cnt[sem] = self.slot_cnt.get(sem, 0) + 16
                slots_used[sem] = self.slot_cnt[sem]
                op.ticket = (sem, self.slot_cnt[sem], 16)
            elif has_dep[i]:
                assert op.eng in self.eng_sem, f"compute op on {op.eng}"
                counts[op.eng] += 1
                op.ticket = (self.eng_sem[op.eng], counts[op.eng], 1)
        waited = {e: {} for e in self.ENGS}
        streams = {e: [] for e in self.ENGS}
        for i, op in enumerate(ops):
            need = {}
            for d in op.deps:
                sem, val, _ = ops[d].ticket
                if need.get(sem, 0) < val:
                    need[sem] = val
            waits = []
            wd = waited[op.eng]
            for sem, val in need.items():
                if wd.get(sem, 0) >= val:
                    continue
                wd[sem] = val
                waits.append((sem, val))
            streams[op.eng].append((waits, op))
        self.bar_count += 1
        bar_val = self.bar_count
        sems = self.sems
        final_waits = [(self.eng_sem[e], counts[e]) for e in last_compute]
        final_waits += list(slots_used.items())

        def make_body(ename):
            def body(eng):
                for waits, op in streams[ename]:
                    for sem, val in waits:
                        eng.wait_ge(sems[sem], val)
                    ins = op.fn(eng)
                    if op.ticket is not None:
                        ins.then_inc(sems[op.ticket[0]], op.ticket[2])
                if ename == "sp":
                    for sem, val in final_waits:
                        eng.wait_ge(sems[sem], val)
                    eng.sem_inc(sems[self.bar_sem], 1)
                else:
                    eng.wait_ge(sems[self.bar_sem], bar_val)
            return body

        with self.nc.Block() as block:
            block.sync(make_body("sp"))
            block.tensor(make_body("pe"))
            block.scalar(make_body("act"))
            block.vector(make_body("dve"))
            block.gpsimd(make_body("pool"))


class Prog:
    def __init__(self, nsub=12, dbg=False):
        self.nsub = nsub
        self.dbg = dbg
        self.nc = nc = bass.Bass("TRN2", target_bir_lowering=False)
        self.din = {}
        self.uid = 0

        def din(name, shape, dt=F32):
            self.din[name] = nc.dram_tensor(name, list(shape), dt, kind="ExternalInput").ap()

        din("x", [NLAT, D]); din("ctx", [NCTX, D]); din("c2", [128, 8, 2])
        din("ada_w", [4, D, 9 * D]); din("ada_b", [4, 9 * D])
        din("ln_g", [4, 3, D]); din("ln_b", [4, 3, D])
        din("ffn_w_in", [4, 2, D, 2 * DFF]); din("ffn_w_out", [4, 2, DFF, D])
        din("ret_w_in", [2, D, 6144]); din("ret_w_out", [2, 2048, D])
        din("na_w_qkv", [1, D, 3072]); din("na_rpb", [1, 16, 15, 31]); din("na_w_out", [1, D, D])
        din("lru_w_in", [1, D, 2048]); din("lru_conv_w", [1, 4, D]); din("lru_conv_b", [1, D])
        din("lru_w_a", [1, 2, 4, 256, 256]); din("lru_b_a", [1, 2, D])
        din("lru_w_x", [1, 2, 4, 256, 256]); din("lru_b_x", [1, 2, D])
        din("lru_lam", [1, 2, D]); din("lru_w_out", [1, D, D])
        din("k_ident", [128, 128]); din("k_cos", [128, 64]); din("k_sin", [128, 64])
        din("k_dec", [128, 16, 128]); din("k_mask", [128, 2, 128])
        din("k_j2", [128, 128]); din("k_namask", [128, 64]); din("rpb_pad", [16, 15, 127]); din("lru_cols", [128, 8, 11])
        self.out = nc.dram_tensor("out", [NLAT, D], F32, kind="ExternalOutput").ap()
        if dbg:
            self.outc = nc.dram_tensor("outc", [NCTX, D], F32, kind="ExternalOutput").ap()
        self.Hd = nc.dram_tensor("Hd", [NTOK, D], F32, kind="Internal").ap()
        self.MODS = nc.dram_tensor("MODS", [4, 2, 9 * D], F32, kind="Internal").ap()
        self.GT = nc.dram_tensor("GT", [DFF, NTOK], BF16, kind="Internal").ap()
        self.QT = nc.dram_tensor("QT", [2, D, NTOK], BF16, kind="Internal").ap()
        self.KT = nc.dram_tensor("KT", [2, D, NTOK], BF16, kind="Internal").ap()
        self.Vd = nc.dram_tensor("Vd", [NTOK, 2048], BF16, kind="Internal").ap()
        self.SGd = nc.dram_tensor("SGd", [NTOK, 2048], F32, kind="Internal").ap()
        self.Od = nc.dram_tensor("Od", [2, NTOK, 2048], F32, kind="Internal").ap()
        self.XR = nc.dram_tensor("XR", [D, NTOK], F32, kind="Internal").ap()
        self.GG = nc.dram_tensor("GG", [D, NTOK], F32, kind="Internal").ap()
        self.YT = nc.dram_tensor("YT", [D, NTOK], BF16, kind="Internal").ap()

    def sb(self, st, name, shape, dt):
        self.uid += 1
        return st.enter_context(self.nc.sbuf_tensor(f"{name}_{self.uid}", list(shape), dt))

    def ps(self, st, name, shape, dt):
        self.uid += 1
        return st.enter_context(self.nc.psum_tensor(f"{name}_{self.uid}", list(shape), dt))

    def MM(self, out, lhsT, rhs, start, stop, reads, writes):
        self.S.add("pe", lambda e: e.matmul(out, lhsT, rhs, start=start, stop=stop), reads, writes)

    def TR(self, out, in_, reads, writes):
        np_ = in_.shape[0]
        idap = self.ident[:np_, :np_]
        self.S.add("pe", lambda e: e.transpose(out, in_, idap), reads, writes)

    def TT(self, eng, out, in0, in1, op, reads, writes):
        self.S.add(eng, lambda e: e.tensor_tensor(out=out, in0=in0, in1=in1, op=op), reads, writes)

    def TS(self, eng, out, in0, s1, s2, op0, op1, reads, writes):
        if s2 is None:
            self.S.add(eng, lambda e: e.tensor_scalar(out=out, in0=in0, scalar1=s1, scalar2=None, op0=op0),
                       reads, writes)
        else:
            self.S.add(eng, lambda e: e.tensor_scalar(out=out, in0=in0, scalar1=s1, scalar2=s2, op0=op0, op1=op1),
                       reads, writes)

    def STT(self, out, in0, scalar, in1, op0, op1, reads, writes):
        self.S.add("dve", lambda e: e.scalar_tensor_tensor(out=out, in0=in0, scalar=scalar, in1=in1,
                                                           op0=op0, op1=op1), reads, writes)

    def ACT(self, out, in_, func, reads, writes, bias=None, scale=None):
        kw = {}
        if bias is not None:
            kw["bias"] = bias
        if scale is not None:
            kw["scale"] = scale
        self.S.add("act", lambda e: e.activation(out=out, in_=in_, func=func, **kw), reads, writes)

    def CP(self, eng, out, in_, reads, writes):
        if eng == "act":
            self.S.add("act", lambda e: e.copy(out=out, in_=in_), reads, writes)
        else:
            self.S.add(eng, lambda e: e.tensor_copy(out=out, in_=in_), reads, writes)

    def DMA(self, eng, out, in_, reads, writes, slot):
        self.S.dma(eng, out, in_, reads, writes, slot)

    def h_in(self, t):
        if self.sidx == 0:
            if t < 32:
                return self.din["x"][t * 128:(t + 1) * 128, :], []
            return self.din["ctx"][(t - 32) * 128:(t - 31) * 128, :], []
        return self.Hd[t * 128:(t + 1) * 128, :], [("H", t)]

    def h_out(self, t):
        if self.sidx == self.nsub - 1:
            if t < 32:
                return self.out[t * 128:(t + 1) * 128, :]
            if self.dbg:
                return self.outc[(t - 32) * 128:(t - 31) * 128, :]
        return self.Hd[t * 128:(t + 1) * 128, :]

    def build(self):
        nc = self.nc
        with ExitStack() as gst:
            self.S = Sched(nc, gst)
            self.ident = self.sb(gst, "ident", [128, 128], BF16)
            self.DMA("pool", self.ident[:], self.din["k_ident"][:, :], [], ["ident"], "ident")
            self.s2 = self.sb(gst, "s2", [128, 8, 2], BF16)
            self.phase_mod()
            self.sidx = 0
            for L in range(DEPTH):
                last = L == DEPTH - 1
                for sub in range(3):
                    if self.sidx >= self.nsub:
                        break
                    tiles = list(range(32)) if (last and sub == 2 and not self.dbg) else list(range(NT))
                    if sub == 0:
                        self.ffn_sublayer(L, 0, tiles)
                    elif sub == 2:
                        self.ffn_sublayer(L, 1, tiles)
                    else:
                        self.mixer_sublayer(L, tiles)
                    self.sidx += 1
        return nc

    def mod_bufs(self, st):
        return dict(
            wa=[self.sb(st, f"wa{i}", [128, 8, 512], BF16) for i in range(2)],
            bt=[self.sb(st, f"bt{i}", [2, 512], F32) for i in range(2)],
            mo=[self.sb(st, f"mo{i}", [2, 512], F32) for i in range(2)],
            pm=[self.ps(st, f"pm{i}", [2, 512], F32) for i in range(2)])

    def mod_item(self, mb, L, j):
        b = j % 2
        wa, bt, mo, pm = mb["wa"][b], mb["bt"][b], mb["mo"][b], mb["pm"][b]
        s2 = self.s2
        cs = slice(j * 512, (j + 1) * 512)
        self.DMA("pool", wa[:], self.din["ada_w"][L, :, cs].rearrange("(k p) n -> p k n", p=128),
                 [], [("wa", b)], ("wa", b))
        self.DMA("sp", bt[:], self.din["ada_b"][L:L + 1, cs].partition_broadcast(2), [], [("bt", b)], ("bt", b))
        for k in range(8):
            self.MM(pm[:], s2[:, k, :], wa[:, k, :], k == 0, k == 7, ["s2", ("wa", b)], [("pm", b)])
        self.TT("dve", mo[:], pm[:], bt[:], ALU.add, [("pm", b), ("bt", b)], [("mo", b)])
        self.DMA("pool", self.MODS[L, :, cs], mo[:], [("mo", b)], [("MODS", L, j)], ("mo", b))

    def phase_mod(self):
        S = self.S
        with ExitStack() as st:
            c2f = self.sb(st, "c2f", [128, 8, 2], F32)
            self.DMA("sp", c2f[:], self.din["c2"][:, :, :], [], ["c2f"], "c2f")
            self.ACT(self.s2[:], c2f[:], AF.Silu, ["c2f"], ["s2"])
            mb = self.mod_bufs(st)
            for j in range(18):
                self.mod_item(mb, 0, j)
            S.flush()

    def _bc_mod(self, st, bc, L, nm, mi, side):
        t = self.sb(st, f"bc{nm}{side}", [128, D], F32)
        bc[nm, side] = t
        self.DMA("sp", t[:], self.MODS[L, side:side + 1, mi * D:(mi + 1) * D].partition_broadcast(128),
                 [], [("bc", nm, side)], ("bc", nm, side))
        return t

    def load_bc_prep(self, st, L, sub):
        bc = {}
        for side in (0, 1):
            a = self._bc_mod(st, bc, L, "A", 3 * sub + 1, side)
            self._bc_mod(st, bc, L, "B", 3 * sub, side)
            self.TS("dve", a[:], a[:], 1.0, None, ALU.add, None, [("bc", "A", side)], [("bc", "A", side)])
        return bc

    def load_bc_epi(self, st, L, sub, res):
        bc = {}
        for side in (0, 1):
            g = self._bc_mod(st, bc, L, "G", 3 * sub + 2, side)
            if res != 1.0:
                self.TS("dve", g[:], g[:], float(res), None, ALU.mult, None, [("bc", "G", side)], [("bc", "G", side)])
        for nm, src in (("lg", "ln_g"), ("lb", "ln_b")):
            t = self.sb(st, f"bc{nm}", [128, D], F32)
            bc[nm] = t
            self.DMA("sp", t[:], self.din[src][L, sub:sub + 1, :].partition_broadcast(128), [], [("bc", nm)], ("bc", nm))
        return bc

    def prep_alloc(self, st, bc, uT):
        hl = [self.sb(st, f"hl{i}", [128, D], F32) for i in range(2)]
        ut = [self.sb(st, f"ut{i}", [128, D], F32) for i in range(2)]
        ub = [self.sb(st, f"ub{i}", [128, D], BF16) for i in range(2)]
        ptr = [self.ps(st, f"ptr{i}", [128, D], BF16) for i in range(2)]

        def prep_tile(t):
            par = t % 2
            side = 1 if t >= 32 else 0
            src, rk = self.h_in(t)
            self.DMA("sp", hl[par][:], src, rk, [("hl", par)], ("hl", par))
            self.TT("dve", ut[par][:], hl[par][:], bc["A", side][:], ALU.mult,
                    [("hl", par), ("bc", "A", side)], [("ut", par)])
            self.TT("pool", ub[par][:], ut[par][:], bc["B", side][:], ALU.add,
                    [("ut", par), ("bc", "B", side)], [("ub", par)])
            for k in range(8):
                self.TR(ptr[par][:, k * 128:(k + 1) * 128], ub[par][:, k * 128:(k + 1) * 128],
                        [("ub", par), "ident"], [("ptr", par)])
            self.CP("act", uT[:, :, t * 128:(t + 1) * 128], ptr[par][:].rearrange("p (k t) -> p k t", k=8),
                    [("ptr", par)], [("uT", t)])
        return prep_tile

    def prep(self, st, bc, uT, tiles):
        prep_tile = self.prep_alloc(st, bc, uT)
        for t in tiles:
            prep_tile(t)

    def block_prepper(self, prep_tile, blks):
        state = {"n": 0}

        def ensure(nb):
            while state["n"] < min(nb, len(blks)):
                t0, n = blks[state["n"]]
                for t in range(t0 // 128, (t0 + n) // 128):
                    prep_tile(t)
                state["n"] += 1
        return ensure

    def make_ebufs(self, st):
        eb = {}
        for nm in ("eh", "t1", "xx", "hn"):
            eb[nm] = [self.sb(st, f"e{nm}{i}", [128, D], F32) for i in range(2)]
        eb["st6"] = [self.sb(st, f"est6{i}", [128, 2, 6], F32) for i in range(2)]
        eb["mv"] = [self.sb(st, f"emv{i}", [128, 2], F32) for i in range(2)]
        eb["rs"] = [self.sb(st, f"ers{i}", [128, 1], F32) for i in range(2)]
        return eb

    def epilogue(self, t, yap, ykeys, bc, eb):
        par = t % 2
        side = 1 if t >= 32 else 0
        eh, t1, xx, hn = eb["eh"][par], eb["t1"][par], eb["xx"][par], eb["hn"][par]
        st6, mv, rs = eb["st6"][par], eb["mv"][par], eb["rs"][par]
        src, rk = self.h_in(t)
        self.DMA("sp", eh[:], src, rk, [("eh", par)], ("eh", par))
        self.TT("dve", t1[:], yap, bc["G", side][:], ALU.mult, list(ykeys) + [("bc", "G", side)], [("t1", par)])
        self.STT(xx[:], eh[:], float(ALPHA), t1[:], ALU.mult, ALU.add, [("eh", par), ("t1", par)], [("xx", par)])
        for j in range(2):
            self.S.add("dve", lambda e, j=j: e.bn_stats(out=st6[:, j, :], in_=xx[:, j * 512:(j + 1) * 512]),
                       [("xx", par)], [("st6", par)])
        self.S.add("dve", lambda e: e.bn_aggr(out=mv[:], in_=st6[:].rearrange("p a b -> p (a b)")),
                   [("st6", par)], [("mv", par)])
        self.TS("dve", rs[:], mv[:, 1:2], EPS, None, ALU.add, None, [("mv", par)], [("rs", par)])
        self.ACT(rs[:], rs[:], AF.Sqrt, [("rs", par)], [("rs", par)])
        self.S.add("dve", lambda e: e.reciprocal(out=rs[:], in_=rs[:]), [("rs", par)], [("rs", par)])
        self.TS("dve", t1[:], xx[:], mv[:, 0:1], rs[:], ALU.subtract, ALU.mult,
                [("xx", par), ("mv", par), ("rs", par)], [("t1", par)])
        self.TT("pool", hn[:], t1[:], bc["lg"][:], ALU.mult, [("t1", par), ("bc", "lg")], [("hn", par)])
        self.TT("pool", hn[:], hn[:], bc["lb"][:], ALU.add, [("hn", par), ("bc", "lb")], [("hn", par)])
        self.DMA("pool", self.h_out(t), hn[:], [("hn", par)], [("H", t)], ("hn", par))

    def ffn_sublayer(self, L, half, tiles):
        S = self.S
        sub = 0 if half == 0 else 2
        nt = len(tiles)
        blks = [(i * 512, 512) for i in range(8)]
        if nt > 32:
            blks.append((NLAT, NCTX))
        w_in = self.din["ffn_w_in"][L, half]
        w_out = self.din["ffn_w_out"][L, half]
        with ExitStack() as sst:
            wo = self.sb(sst, "wo", [128, 22, D], BF16)
            wparts = [(0, 6), (6, 12), (12, 18), (18, 22)]
            with ExitStack() as st:
                bc = self.load_bc_prep(st, L, sub)
                uT = self.sb(st, "uT", [128, 8, NTOK], BF16)
                ensure = self.block_prepper(self.prep_alloc(st, bc, uT), blks)
                wg = [self.sb(st, f"wg{i}", [128, 8, 256], BF16) for i in range(2)]
                wu = [self.sb(st, f"wu{i}", [128, 8, 256], BF16) for i in range(2)]
                sg = [self.sb(st, f"sg{i}", [128, 512], F32) for i in range(2)]
                gs = [self.sb(st, f"gs{i}", [128, 512], BF16) for i in range(2)]
                pg = [self.ps(st, f"pg{i}", [128, 512], F32) for i in range(2)]
                pu = [self.ps(st, f"pu{i}", [128, 512], F32) for i in range(2)]
                it = 0

                def load_w(g):
                    b = g % 2
                    self.DMA("pool", wg[b][:], w_in[:, g * 256:(g + 1) * 256].rearrange("(k p) n -> p k n", p=128),
                             [], [("wg", b)], ("wg", b))
                    self.DMA("pool", wu[b][:],
                             w_in[:, DFF + g * 256:DFF + (g + 1) * 256].rearrange("(k p) n -> p k n", p=128),
                             [], [("wu", b)], ("wu", b))

                load_w(0)
                for g in range(11):
                    b = g % 2
                    if g + 1 < 11:
                        load_w(g + 1)
                    if 3 <= g <= 6:
                        f0, f1 = wparts[g - 3]
                        self.DMA("pool", wo[:, f0:f1, :],
                                 w_out[f0 * 128:f1 * 128, :].rearrange("(f p) d -> p f d", p=128),
                                 [], [("wo", g - 3)], ("wo", g - 3))
                    for bi, (t0, n) in enumerate(blks):
                        if g == 0:
                            ensure(bi + 2)
                        ukeys = [("uT", t) for t in range(t0 // 128, (t0 + n) // 128)]
                        for fc in range(2):
                            pb = it % 2
                            it += 1
                            for k in range(8):
                                self.MM(pg[pb][:, :n], wg[b][:, k, fc * 128:(fc + 1) * 128], uT[:, k, t0:t0 + n],
                                        k == 0, k == 7, [("wg", b)] + ukeys, [("pg", pb)])
                            for k in range(8):
                                self.MM(pu[pb][:, :n], wu[b][:, k, fc * 128:(fc + 1) * 128], uT[:, k, t0:t0 + n],
                                        k == 0, k == 7, [("wu", b)] + ukeys, [("pu", pb)])
                            self.ACT(sg[pb][:, :n], pg[pb][:, :n], AF.Silu, [("pg", pb)], [("sg", pb)])
                            self.TT("dve", gs[pb][:, :n], pu[pb][:, :n], sg[pb][:, :n], ALU.mult,
                                    [("pu", pb), ("sg", pb)], [("gs", pb)])
                            r0 = g * 256 + fc * 128
                            self.DMA("pool", self.GT[r0:r0 + 128, t0:t0 + n], gs[pb][:, :n],
                                     [("gs", pb)], [("GT", g, fc, bi)], ("gs", pb))
                S.flush()
            with ExitStack() as st:
                bc = self.load_bc_epi(st, L, sub, 0.5)
                mod_todo = list(range(18)) if (half == 1 and L + 1 < DEPTH) else []
                mb = self.mod_bufs(st) if mod_todo else None
                gb = [self.sb(st, f"gb{i}", [128, 22, 512], BF16) for i in range(2)]
                eb = self.make_ebufs(st)
                py = [self.ps(st, f"py{i}", [128, D], F32) for i in range(2)]
                for bi, (t0, n) in enumerate(blks):
                    b = bi % 2
                    self.DMA("sp", gb[b][:, :, :n], self.GT[:, t0:t0 + n].rearrange("(f p) t -> p f t", p=128),
                             [], [("gb", b)], ("gb", b))
                    for ti in range(n // 128):
                        t = t0 // 128 + ti
                        par = t % 2
                        for f in range(22):
                            pi = [i for i, (f0, f1) in enumerate(wparts) if f0 <= f < f1][0]
                            for hh in range(2):
                                self.MM(py[par][:, hh * 512:(hh + 1) * 512], gb[b][:, f, ti * 128:(ti + 1) * 128],
                                        wo[:, f, hh * 512:(hh + 1) * 512], f == 0, f == 21,
                                        [("gb", b)], [("py", par)])
                        if mod_todo:
                            self.mod_item(mb, L + 1, mod_todo.pop(0))
                        self.epilogue(t, py[par][:], [("py", par)], bc, eb)
                S.flush()

    def mixer_sublayer(self, L, tiles):
        kind, idx = L % 3, L // 3
        if kind == 0:
            self.ret_sublayer(L, idx, tiles)
        elif kind == 1:
            self.na_sublayer(L, idx, tiles)
        else:
            self.lru_sublayer(L, idx, tiles)

    def ret_sublayer(self, L, idx, tiles):
        S = self.S
        w_in = self.din["ret_w_in"][idx]
        w_out = self.din["ret_w_out"][idx]
        blks = [(i * 512, 512) for i in range(8)] + [(NLAT, NCTX)]
        alltiles = list(range(NT))
        with ExitStack() as ust:
          uT = self.sb(ust, "uT", [128, 8, NTOK], BF16)
          with ExitStack() as st:
            bc = self.load_bc_prep(st, L, 1)
            ensure = self.block_prepper(self.prep_alloc(st, bc, uT), blks)
            cos = self.sb(st, "cos", [128, 64], F32)
            sin = self.sb(st, "sin", [128, 64], F32)
            dec = self.sb(st, "dec", [128, 16, 128], F32)
            self.DMA("sp", cos[:], self.din["k_cos"][:, :], [], ["cos"], "cos")
            self.DMA("sp", sin[:], self.din["k_sin"][:, :], [], ["sin"], "sin")
            self.DMA("sp", dec[:], self.din["k_dec"][:, :, :], [], ["dec"], "dec")
            wq = [self.sb(st, f"wq{i}", [128, 8, 128], BF16) for i in range(2)]
            wqs = [self.sb(st, f"wqs{i}", [128, 8, 128], BF16) for i in range(2)]
            t1 = [self.sb(st, f"rt1{i}", [128, 512], F32) for i in range(2)]
            t2 = [self.sb(st, f"rt2{i}", [128, 512], F32) for i in range(2)]
            rot = [self.sb(st, f"rot{i}", [128, 512], F32) for i in range(2)]
            qst = [self.sb(st, f"qst{i}", [128, 512], BF16) for i in range(4)]
            pq = [self.ps(st, f"pq{i}", [128, 512], F32) for i in range(2)]
            pqs = [self.ps(st, f"pqs{i}", [128, 512], F32) for i in range(2)]

            def load_wq(c16):
                b = c16 % 2
                col0 = (1024 if c16 >= 8 else 0) + (c16 % 8) * 128
                self.DMA("pool", wq[b][:], w_in[:, col0:col0 + 128].rearrange("(k p) n -> p k n", p=128),
                         [], [("wq", b)], ("wq", b))
                self.CP("act", wqs[b][:, :, 0:64], wq[b][:, :, 64:128], [("wq", b)], [("wqs", b)])
                self.CP("act", wqs[b][:, :, 64:128], wq[b][:, :, 0:64], [("wq", b)], [("wqs", b)])

            it = 0
            load_wq(0)
            for c16 in range(16):
                b = c16 % 2
                isk = c16 >= 8
                c = c16 % 8
                h = c // 2
                if c16 + 1 < 16:
                    load_wq(c16 + 1)
                dst = self.KT if isk else self.QT
                for bi, (t0, n) in enumerate(blks):
                    if c16 == 0:
                        ensure(bi + 2)
                    lat = t0 < NLAT
                    pb = it % 2
                    it += 1
                    ukeys = [("uT", t) for t in range(t0 // 128, (t0 + n) // 128)]
                    for k in range(8):
                        self.MM(pq[pb][:, :n], wq[b][:, k, :], uT[:, k, t0:t0 + n], k == 0, k == 7,
                                [("wq", b)] + ukeys, [("pq", pb)])
                    if lat:
                        for k in range(8):
                            self.MM(pqs[pb][:, :n], wqs[b][:, k, :], uT[:, k, t0:t0 + n], k == 0, k == 7,
                                    [("wqs", b)] + ukeys, [("pqs", pb)])
                        nr = n // 64
                        r0 = t0 // 64
                        if c % 2 == 0:
                            cosap = cos[:, r0:r0 + nr].unsqueeze(2).to_broadcast([128, nr, 64])
                            sinap = sin[:, r0:r0 + nr].unsqueeze(2).to_broadcast([128, nr, 64])
                        else:
                            cosap = cos[:, :].unsqueeze(1).to_broadcast([128, nr, 64])
                            sinap = sin[:, :].unsqueeze(1).to_broadcast([128, nr, 64])
                        v3 = lambda ap: ap.rearrange("p (r c) -> p r c", c=64)
                        self.TT("dve", v3(t1[pb][:, :n]), v3(pq[pb][:, :n]), cosap, ALU.mult,
                                [("pq", pb), "cos"], [("t1", pb)])
                        self.TT("dve", v3(t2[pb][:, :n]), v3(pqs[pb][:, :n]), sinap, ALU.mult,
                                [("pqs", pb), "sin"], [("t2", pb)])
                        self.TT("pool", rot[pb][:, :n], t1[pb][:, :n], t2[pb][:, :n], ALU.add,
                                [("t1", pb), ("t2", pb)], [("rot", pb)])
                    else:
                        self.CP("act", rot[pb][:, :n], pq[pb][:, :n], [("pq", pb)], [("rot", pb)])
                    for dr in range(2):
                        di = dr * 8 + (4 if isk else 0) + h
                        nch = n // 128
                        decap = dec[:, di, :].unsqueeze(1).to_broadcast([128, nch, 128])
                        v4 = lambda ap: ap.rearrange("p (a b) -> p a b", b=128)
                        qb = dr * 2 + pb
                        self.TT("dve" if dr == 0 else "pool", v4(qst[qb][:, :n]), v4(rot[pb][:, :n]), decap, ALU.mult,
                                [("rot", pb), "dec"], [("qst", qb)])
                        self.DMA("pool", dst[dr, c * 128:(c + 1) * 128, t0:t0 + n], qst[qb][:, :n],
                                 [("qst", qb)], [("QK", c16, bi, dr)], ("qst", qb))
            S.flush()
          with ExitStack() as st:
            wv = self.sb(st, "wv", [128, 8, 2048], BF16)
            wgg = self.sb(st, "wgg", [128, 8, 2048], BF16)
            for i in range(4):
                self.DMA("pool", wv[:, :, i * 512:(i + 1) * 512],
                         w_in[:, 2048 + i * 512:2048 + (i + 1) * 512].rearrange("(k p) n -> p k n", p=128),
                         [], [("wv", i)], ("wv", i))
            for i in range(4):
                self.DMA("pool", wgg[:, :, i * 512:(i + 1) * 512],
                         w_in[:, 4096 + i * 512:4096 + (i + 1) * 512].rearrange("(k p) n -> p k n", p=128),
                         [], [("wgg", i)], ("wgg", i))
            vst = [self.sb(st, f"vst{i}", [128, 2048], BF16) for i in range(2)]
            sgst = [self.sb(st, f"sgst{i}", [128, 2048], F32) for i in range(2)]
            pv = [self.ps(st, f"pv{i}", [128, 512], F32) for i in range(2)]
            it = 0
            for t in alltiles:
                par = t % 2
                for g4 in range(4):
                    pb = it % 2
                    it += 1
                    for k in range(8):
                        self.MM(pv[pb][:], uT[:, k, t * 128:(t + 1) * 128], wv[:, k, g4 * 512:(g4 + 1) * 512],
                                k == 0, k == 7, [("uT", t), ("wv", g4)], [("pv", pb)])
                    self.CP("act", vst[par][:, g4 * 512:(g4 + 1) * 512], pv[pb][:], [("pv", pb)], [("vst", par)])
                self.DMA("pool", self.Vd[t * 128:(t + 1) * 128, :], vst[par][:], [("vst", par)], [("Vd", t)], ("vst", par))
                for g4 in range(4):
                    pb = it % 2
                    it += 1
                    for k in range(8):
                        self.MM(pv[pb][:], uT[:, k, t * 128:(t + 1) * 128], wgg[:, k, g4 * 512:(g4 + 1) * 512],
                                k == 0, k == 7, [("uT", t), ("wgg", g4)], [("pv", pb)])
                    self.ACT(sgst[par][:, g4 * 512:(g4 + 1) * 512], pv[pb][:], AF.Silu, [("pv", pb)], [("sgst", par)])
                self.DMA("pool", self.SGd[t * 128:(t + 1) * 128, :], sgst[par][:], [("sgst", par)], [("SGd", t)],
                         ("sgst", par))
            S.flush()
        with ExitStack() as st:
            msk = self.sb(st, "msk", [128, 2, 128], F32)
            self.DMA("sp", msk[:], self.din["k_mask"][:, :, :], [], ["msk"], "msk")
            T = self.sb(st, "Tst", [128, 16, 512], F32)
            Sb = self.sb(st, "Sbf", [128, 16, 512], BF16)
            self.S.add("dve", lambda e: e.memset(T[:], 0.0), [], [("T", i) for i in range(16)])
            self.S.add("dve", lambda e: e.memset(Sb[:], 0.0), [], [("Sb", i) for i in range(16)])
            qt = [self.sb(st, f"qt{i}", [128, 8, 128], BF16) for i in range(4)]
            kt = [self.sb(st, f"kt{i}", [128, 8, 128], BF16) for i in range(4)]
            vt = [self.sb(st, f"vt{i}", [128, 2048], BF16) for i in range(4)]
            ktok = [self.sb(st, f"ktok{i}", [128, 1024], BF16) for i in range(2)]
            attm = [self.sb(st, f"attm{i}", [128, 128], BF16) for i in range(2)]
            ost = [self.sb(st, f"ost{i}", [128, 2048], F32) for i in range(4)]
            pat = [self.ps(st, f"pat{i}", [128, 512], F32) for i in range(2)]
            po = [self.ps(st, f"po{i}", [128, 512], F32) for i in range(2)]
            pk = [self.ps(st, f"pk{i}", [128, 512], F32) for i in range(2)]
            pt = self.ps(st, "ptk", [128, 1024], BF16)
            order = [[32, 33] + list(range(32)), [33, 32] + list(range(31, -1, -1))]
            ia = 0
            io = 0
            ik = 0
            for s_ in range(NT):
                for dr in range(2):
                    t = order[dr][s_]
                    bi = dr * 2 + s_ % 2
                    kb = (s_ * 2 + dr) % 2
                    ts = slice(t * 128, (t + 1) * 128)
                    self.DMA("sp", qt[bi][:], self.QT[dr, :, ts].rearrange("(c p) t -> p c t", p=128),
                             [], [("qt", bi)], ("qt", bi))
                    self.DMA("sp", kt[bi][:], self.KT[dr, :, ts].rearrange("(c p) t -> p c t", p=128),
                             [], [("kt", bi)], ("kt", bi))
                    self.DMA("sp", vt[bi][:], self.Vd[ts, :], [], [("vt", bi)], ("vt", bi))
                    for c in range(8):
                        self.TR(pt[:, c * 128:(c + 1) * 128], kt[bi][:, c, :], [("kt", bi), "ident"], ["ptk"])
                    self.CP("act", ktok[kb][:], pt[:], ["ptk"], [("ktok", kb)])
                    for h in range(4):
                        gh = h if dr == 0 else 3 - h
                        ch = float((1.0 - 2.0 ** (-5 - gh)) ** 128)
                        ab = ia % 2
                        ia += 1
                        ob = io % 2
                        io += 1
                        for cc in range(2):
                            self.MM(pat[ab][:, :128], kt[bi][:, 2 * h + cc, :], qt[bi][:, 2 * h + cc, :],
                                    cc == 0, cc == 1, [("kt", bi), ("qt", bi)], [("pat", ab)])
                        self.TT("dve", attm[ab][:], pat[ab][:, :128], msk[:, dr, :], ALU.mult,
                                [("pat", ab), "msk"], [("attm", ab)])
                        vh = vt[bi][:, h * 512:(h + 1) * 512]
                        self.MM(po[ob][:], attm[ab][:], vh, True, False, [("attm", ab), ("vt", bi)], [("po", ob)])
                        for cc in range(2):
                            si = (dr * 4 + h) * 2 + cc
                            self.MM(po[ob][:], qt[bi][:, 2 * h + cc, :], Sb[:, si, :], False, cc == 1,
                                    [("qt", bi), ("Sb", si)], [("po", ob)])
                        self.CP("act", ost[bi][:, h * 512:(h + 1) * 512], po[ob][:], [("po", ob)], [("ost", bi)])
                        for cc in range(2):
                            si = (dr * 4 + h) * 2 + cc
                            kb2 = ik % 2
                            ik += 1
                            self.MM(pk[kb2][:], ktok[kb][:, (2 * h + cc) * 128:(2 * h + cc + 1) * 128], vh, True, True,
                                    [("ktok", kb), ("vt", bi)], [("pk", kb2)])
                            self.STT(T[:, si, :], T[:, si, :], ch, pk[kb2][:], ALU.mult, ALU.add,
                                     [("T", si), ("pk", kb2)], [("T", si)])
                            self.TS("pool", Sb[:, si, :], T[:, si, :], ch, None, ALU.mult, None, [("T", si)], [("Sb", si)])
                    self.DMA("pool", self.Od[dr, ts, :], ost[bi][:], [("ost", bi)], [("Od", dr, t)], ("ost", bi))
            S.flush()
        with ExitStack() as st:
            bc = self.load_bc_epi(st, L, 1, 1.0)
            wo = self.sb(st, "rwo", [128, 16, D], BF16)
            for i in range(4):
                self.DMA("pool", wo[:, i * 4:(i + 1) * 4, :],
                         w_out[i * 512:(i + 1) * 512, :].rearrange("(f p) d -> p f d", p=128),
                         [], [("wo", i)], ("wo", i))
            of = [self.sb(st, f"of{i}", [128, 2048], F32) for i in range(2)]
            obk = [self.sb(st, f"obk{i}", [128, 2048], F32) for i in range(2)]
            sgl = [self.sb(st, f"sgl{i}", [128, 2048], F32) for i in range(2)]
            obf = [self.sb(st, f"obf{i}", [128, 2048], BF16) for i in range(2)]
            oT = [self.sb(st, f"oT{i}", [128, 16, 128], BF16) for i in range(2)]
            gst6 = [self.sb(st, f"gst6{i}", [128, 4, 6], F32) for i in range(2)]
            gmv = [self.sb(st, f"gmv{i}", [128, 4, 2], F32) for i in range(2)]
            grs = [self.sb(st, f"grs{i}", [128, 4], F32) for i in range(2)]
            eb = self.make_ebufs(st)
            ptr = self.ps(st, "ptro", [128, 2048], BF16)
            py = [self.ps(st, f"py{i}", [128, D], F32) for i in range(2)]
            for t in tiles:
                par = t % 2
                ts = slice(t * 128, (t + 1) * 128)
                self.DMA("sp", of[par][:], self.Od[0, ts, :], [], [("of", par)], ("of", par))
                self.DMA("sp", obk[par][:], self.Od[1, ts, :], [], [("obk", par)], ("obk", par))
                self.DMA("sp", sgl[par][:], self.SGd[ts, :], [], [("sgl", par)], ("sgl", par))
                self.TT("pool", of[par][:], of[par][:], obk[par][:], ALU.add, [("of", par), ("obk", par)], [("of", par)])
                for h in range(4):
                    self.S.add("dve", lambda e, h=h, par=par: e.bn_stats(out=gst6[par][:, h, :],
                                                                        in_=of[par][:, h * 512:(h + 1) * 512]),
                               [("of", par)], [("gst6", par)])
                for h in range(4):
                    self.S.add("dve", lambda e, h=h, par=par: e.bn_aggr(out=gmv[par][:, h, :], in_=gst6[par][:, h, :]),
                               [("gst6", par)], [("gmv", par)])
                self.TS("dve", grs[par][:], gmv[par][:, :, 1], EPS, None, ALU.add, None, [("gmv", par)], [("grs", par)])
                self.ACT(grs[par][:], grs[par][:], AF.Sqrt, [("grs", par)], [("grs", par)])
                self.S.add("dve", lambda e, par=par: e.reciprocal(out=grs[par][:], in_=grs[par][:]),
                           [("grs", par)], [("grs", par)])
                for h in range(4):
                    hs = slice(h * 512, (h + 1) * 512)
                    self.TS("dve", of[par][:, hs], of[par][:, hs], gmv[par][:, h, 0:1], grs[par][:, h:h + 1],
                            ALU.subtract, ALU.mult, [("of", par), ("gmv", par), ("grs", par)], [("of", par)])
                self.TT("pool", obf[par][:], of[par][:], sgl[par][:], ALU.mult, [("of", par), ("sgl", par)], [("obf", par)])
                for f in range(16):
                    self.TR(ptr[:, f * 128:(f + 1) * 128], obf[par][:, f * 128:(f + 1) * 128],
                            [("obf", par), "ident"], ["ptro"])
                self.CP("act", oT[par][:].rearrange("p f t -> p (f t)"), ptr[:], ["ptro"], [("oT", par)])
                for f in range(16):
                    for hh in range(2):
                        self.MM(py[par][:, hh * 512:(hh + 1) * 512], oT[par][:, f, :], wo[:, f, hh * 512:(hh + 1) * 512],
                                f == 0, f == 15, [("oT", par), ("wo", f // 4)], [("py", par)])
                self.epilogue(t, py[par][:], [("py", par)], bc, eb)
            S.flush()

    def na_sublayer(self, L, idx, tiles):
        S = self.S
        w_qkv = self.din["na_w_qkv"][idx]
        w_out = self.din["na_w_out"][idx]
        rp = self.din["rpb_pad"]
        blks = [(i * 512, 512) for i in range(8)] + [(NLAT, NCTX)]
        alltiles = list(range(NT))
        with ExitStack() as ost_:
          OT = self.sb(ost_, "OT", [128, 8, NTOK], BF16)
          with ExitStack() as ust:
            uT = self.sb(ust, "uT", [128, 8, NTOK], BF16)
            with ExitStack() as st:
                bc = self.load_bc_prep(st, L, 1)
                self.prep(st, bc, uT, alltiles)
                S.flush()
            with ExitStack() as st:
                qTc = self.sb(st, "qTc", [128, NTOK], BF16)
                kTc = self.sb(st, "kTc", [128, NTOK], BF16)
                Ve = self.sb(st, "Ve", [128, NT, 2, 65], BF16)
                Vo = self.sb(st, "Vo", [128, 31, 2, 65], BF16)
                BT = self.sb(st, "BT", [128, 2, 15, 64], F32)
                BTr = self.sb(st, "BTr", [128, 2, 15, 64], F32)
                J2 = self.sb(st, "J2", [128, 128], F32)
                M = self.sb(st, "nam", [128, 64], F32)
                self.DMA("sp", J2[:], self.din["k_j2"][:, :], [], ["J2"], "J2")
                wq = [self.sb(st, f"nwq{i}", [128, 8, 128], BF16) for i in range(2)]
                wk = [self.sb(st, f"nwk{i}", [128, 8, 128], BF16) for i in range(2)]
                wv = [self.sb(st, f"nwv{i}", [128, 8, 128], BF16) for i in range(2)]
                sbs = [self.sb(st, f"sbs{i}", [128, 256], F32) for i in range(2)]
                pT = [self.sb(st, f"pT{i}", [128, 384], BF16) for i in range(2)]
                rec = [self.sb(st, f"rec{i}", [128, 2], F32) for i in range(2)]
                otok = [self.sb(st, f"otok{i}", [128, 2, 64], BF16) for i in range(2)]
                pp = [self.ps(st, f"npp{i}", [128, 512], F32) for i in range(2)]
                pss = [self.ps(st, f"pss{i}", [128, 512], F32) for i in range(2)]
                ppo = [self.ps(st, f"ppo{i}", [128, 2, 65], F32) for i in range(2)]
                ptr = [self.ps(st, f"nptr{i}", [128, 128], BF16) for i in range(2)]
                self.DMA("sp", M[:], self.din["k_namask"][:, :], [], ["nam"], "nam")
                self.S.add("dve", lambda e: e.memset(Ve[:], 1.0), [], ["Ve"])
                self.S.add("dve", lambda e: e.memset(Vo[:], 1.0), [], ["Vo"])
                self.S.add("dve", lambda e: e.memset(BTr[:], 0.0), [], ["BTr"])

                def load_w(c):
                    b = c % 2
                    for j, (w, nm) in enumerate(((wq, "nwq"), (wk, "nwk"), (wv, "nwv"))):
                        c0 = j * 1024 + c * 128
                        self.DMA("pool", w[b][:], w_qkv[:, c0:c0 + 128].rearrange("(k p) n -> p k n", p=128),
                                 [], [(nm, b)], (nm, b))

                cnt = {"pp": 0, "ss": 0, "po": 0, "tr": 0}

                def attend(c, q0, nq, chunks):
                    po_i = cnt["po"] % 2
                    cnt["po"] += 1
                    nloc = sum(1 for ch in chunks if ch[3] is not None)
                    nch = len(chunks)
                    for hh in range(2):
                        hp = slice(hh * 64, (hh + 1) * 64)
                        si = cnt["ss"] % 2
                        cnt["ss"] += 1
                        for m, (k0, vb, vt_, rr) in enumerate(chunks):
                            self.MM(pss[si][:, m * nq:(m + 1) * nq], kTc[hp, k0:k0 + 128], qTc[hp, q0:q0 + nq],
                                    True, True, ["kTc", "qTc"], [("pss", si)])
                        if nloc:
                            rr0 = chunks[0][3]
                            self.STT(sbs[si][:, :nloc * nq].rearrange("p (m q) -> p m q", q=nq),
                                     pss[si][:, :nloc * nq].rearrange("p (m q) -> p m q", q=nq), 0.125,
                                     BT[:, hh, rr0:rr0 + 2 * nloc:2, :], ALU.mult, ALU.add,
                                     [("pss", si), "BT"], [("sbs", si)])
                            self.ACT(pT[si][:, :nloc * nq], sbs[si][:, :nloc * nq], AF.Exp, [("sbs", si)], [("pT", si)])
                        self.ACT(pT[si][:, nloc * nq:nch * nq], pss[si][:, nloc * nq:nch * nq], AF.Exp,
                                 [("pss", si)], [("pT", si)], scale=0.125)
                        for m, (k0, vb, vt_, rr) in enumerate(chunks):
                            self.MM(ppo[po_i][:nq, hh, :], pT[si][:, m * nq:(m + 1) * nq], vb[:, vt_, hh, :],
                                    m == 0, m == nch - 1, [("pT", si), "Ve", "Vo"], [("ppo", po_i)])
                    self.S.add("dve", lambda e: e.reciprocal(out=rec[po_i][:nq, :], in_=ppo[po_i][:nq, :, 64]),
                               [("ppo", po_i)], [("rec", po_i)])
                    self.TT("dve", otok[po_i][:nq, :, :], ppo[po_i][:nq, :, 0:64],
                            rec[po_i][:nq, :].unsqueeze(2).to_broadcast([nq, 2, 64]), ALU.mult,
                            [("ppo", po_i), ("rec", po_i)], [("otok", po_i)])
                    ti = cnt["tr"] % 2
                    cnt["tr"] += 1
                    self.TR(ptr[ti][:, :nq], otok[po_i][:nq, :, :].rearrange("q a b -> q (a b)"),
                            [("otok", po_i), "ident"], [("nptr", ti)])
                    self.CP("act", OT[:, c, q0:q0 + nq], ptr[ti][:, :nq], [("nptr", ti)], [("OT", c, q0)])

                load_w(0)
                for c in range(8):
                    b = c % 2
                    if c + 1 < 8:
                        load_w(c + 1)
                    for par in range(2):
                        for hh in range(2):
                            src = bass.AP(tensor=rp.tensor, offset=((2 * c + hh) * 15 + par) * 127,
                                          ap=[[1, 64], [127, 15 - par], [1, 64]])
                            self.DMA("sp", BTr[par * 64:(par + 1) * 64, hh, 0:15 - par, :], src, [], ["BTr"],
                                     ("BTr", par, hh))
                    btr2 = BTr[:].rearrange("p a r q -> p (a r q)")
                    bt2 = BT[:].rearrange("p a r q -> p (a r q)")
                    for (c0_, c1_) in ((0, 512), (512, 1024), (1024, 1536), (1536, 1920)):
                        pi = cnt["pp"] % 2
                        cnt["pp"] += 1
                        nn = c1_ - c0_
                        self.MM(pp[pi][:, :nn], J2[:], btr2[:, c0_:c1_], True, True, ["BTr", "J2"], [("npp", pi)])
                        self.TT("dve", bt2[:, c0_:c1_].rearrange("p (a q) -> p a q", q=64),
                                pp[pi][:, :nn].rearrange("p (a q) -> p a q", q=64),
                                M[:, :].unsqueeze(1).to_broadcast([128, nn // 64, 64]), ALU.add,
                                [("npp", pi), "nam"], ["BT"])
                    for bi, (t0, n) in enumerate(blks):
                        ukeys = [("uT", t) for t in range(t0 // 128, (t0 + n) // 128)]
                        for w, nm, dstT in ((wq, "nwq", qTc), (wk, "nwk", kTc)):
                            pi = cnt["pp"] % 2
                            cnt["pp"] += 1
                            for k in range(8):
                                self.MM(pp[pi][:, :n], w[b][:, k, :], uT[:, k, t0:t0 + n], k == 0, k == 7,
                                        [(nm, b)] + ukeys, [("npp", pi)])
                            self.CP("act", dstT[:, t0:t0 + n], pp[pi][:, :n], [("npp", pi)],
                                    ["qTc" if dstT is qTc else "kTc"])
                    for vb, ntile, toff, nm in ((Ve, NT, 0, "Ve"), (Vo, 31, 64, "Vo")):
                        for g0 in range(0, ntile, 4):
                            g1 = min(ntile, g0 + 4)
                            pi = cnt["pp"] % 2
                            cnt["pp"] += 1
                            for a in range(g0, g1):
                                tk = toff + a * 128
                                for k in range(8):
                                    self.MM(pp[pi][:, (a - g0) * 128:(a - g0 + 1) * 128], uT[:, k, tk:tk + 128],
                                            wv[b][:, k, :], k == 0, k == 7,
                                            [("nwv", b), ("uT", tk // 128), ("uT", min(NT - 1, (tk + 127) // 128))],
                                            [("npp", pi)])
                            ng = g1 - g0
                            self.CP("act", vb[:, g0:g1, :, 0:64],
                                    pp[pi][:, :ng * 128].rearrange("p (a h d) -> p a h d", h=2, d=64),
                                    [("npp", pi)], [nm])
                    for r in range(64):
                        rs = min(max(r - 4, 0), 56)
                        off = rs - r + 7
                        chunks = []
                        for m in range(4):
                            row = rs + 2 * m
                            if row % 2 == 0:
                                chunks.append((row * 64, Ve, row // 2, off + 2 * m))
                            else:
                                chunks.append((row * 64, Vo, (row - 1) // 2, off + 2 * m))
                        chunks.append((NLAT, Ve, 32, None))
                        chunks.append((NLAT + 128, Ve, 33, None))
                        attend(c, r * 64, 64, chunks)
                    if len(tiles) > 32:
                        for qg in range(2):
                            attend(c, NLAT + qg * 128, 128, [(NLAT, Ve, 32, None), (NLAT + 128, Ve, 33, None)])
                S.flush()
          with ExitStack() as st:
            bc = self.load_bc_epi(st, L, 1, 1.0)
            wo = self.sb(st, "nwo", [128, 8, D], BF16)
            for i in range(2):
                self.DMA("pool", wo[:, i * 4:(i + 1) * 4, :],
                         w_out[i * 512:(i + 1) * 512, :].rearrange("(f p) d -> p f d", p=128),
                         [], [("wo", i)], ("wo", i))
            eb = self.make_ebufs(st)
            py = [self.ps(st, f"py{i}", [128, D], F32) for i in range(2)]
            for t in tiles:
                par = t % 2
                for f in range(8):
                    for hh in range(2):
                        self.MM(py[par][:, hh * 512:(hh + 1) * 512], OT[:, f, t * 128:(t + 1) * 128],
                                wo[:, f, hh * 512:(hh + 1) * 512], f == 0, f == 7, [("wo", f // 4)], [("py", par)])
                self.epilogue(t, py[par][:], [("py", par)], bc, eb)
            S.flush()

    def lru_sublayer(self, L, idx, tiles):
        S = self.S
        w_in = self.din["lru_w_in"][idx]
        w_out = self.din["lru_w_out"][idx]
        blks = [(i * 512, 512) for i in range(8)] + [(NLAT, NCTX)]
        alltiles = list(range(NT))
        LAT0, CTX0, XW = 2, NLAT + 5, NTOK + 6
        with ExitStack() as st:
            bc = self.load_bc_prep(st, L, 1)
            uT = self.sb(st, "uT", [128, 8, NTOK], BF16)
            ensure = self.block_prepper(self.prep_alloc(st, bc, uT), blks)
            wx = [self.sb(st, f"lwx{i}", [128, 8, 128], BF16) for i in range(2)]
            wg = [self.sb(st, f"lwg{i}", [128, 8, 128], BF16) for i in range(2)]
            xs = [self.sb(st, f"lxs{i}", [128, 512], F32) for i in range(2)]
            gs = [self.sb(st, f"lgs{i}", [128, 512], F32) for i in range(2)]
            px = [self.ps(st, f"lpx{i}", [128, 512], F32) for i in range(2)]
            pg = [self.ps(st, f"lpg{i}", [128, 512], F32) for i in range(2)]

            def load_w(c):
                b = c % 2
                self.DMA("pool", wg[b][:], w_in[:, c * 128:(c + 1) * 128].rearrange("(k p) n -> p k n", p=128),
                         [], [("lwg", b)], ("lwg", b))
                self.DMA("pool", wx[b][:], w_in[:, D + c * 128:D + (c + 1) * 128].rearrange("(k p) n -> p k n", p=128),
                         [], [("lwx", b)], ("lwx", b))

            it = 0
            load_w(0)
            for c in range(8):
                b = c % 2
                if c + 1 < 8:
                    load_w(c + 1)
                for bi, (t0, n) in enumerate(blks):
                    if c == 0:
                        ensure(bi + 2)
                    pb = it % 2
                    it += 1
                    ukeys = [("uT", t) for t in range(t0 // 128, (t0 + n) // 128)]
                    for k in range(8):
                        self.MM(px[pb][:, :n], wx[b][:, k, :], uT[:, k, t0:t0 + n], k == 0, k == 7,
                                [("lwx", b)] + ukeys, [("lpx", pb)])
                    for k in range(8):
                        self.MM(pg[pb][:, :n], wg[b][:, k, :], uT[:, k, t0:t0 + n], k == 0, k == 7,
                                [("lwg", b)] + ukeys, [("lpg", pb)])
                    self.CP("dve", xs[pb][:, :n], px[pb][:, :n], [("lpx", pb)], [("lxs", pb)])
                    self.ACT(gs[pb][:, :n], pg[pb][:, :n], AF.Gelu, [("lpg", pb)], [("lgs", pb)])
                    self.DMA("pool", self.XR[c * 128:(c + 1) * 128, t0:t0 + n], xs[pb][:, :n], [("lxs", pb)],
                             [("XR", c, bi)], ("lxs", pb))
                    self.DMA("pool", self.GG[c * 128:(c + 1) * 128, t0:t0 + n], gs[pb][:, :n], [("lgs", pb)],
                             [("GG", c, bi)], ("lgs", pb))
            S.flush()
        with ExitStack() as st:
            cols = self.sb(st, "lcols", [128, 8, 11], F32)
            self.DMA("sp", cols[:], self.din["lru_cols"][:, :, :], [], ["lcols"], "lcols")
            sp8 = self.sb(st, "lsp8", [128, 8, 2], F32)
            tA = self.sb(st, "ltA", [128, 8, 2], F32)
            tB = self.sb(st, "ltB", [128, 8, 2], F32)
            lam = cols[:, :, 9:11]
            self.TS("dve", tB[:], lam, -1.0, None, ALU.mult, None, ["lcols"], ["ltB"])
            self.TT("dve", tA[:], lam, tB[:], ALU.max, ["lcols", "ltB"], ["ltA"])
            self.ACT(tA[:], tA[:], AF.Exp, ["ltA"], ["ltA"], scale=-1.0)
            self.ACT(tA[:], tA[:], AF.Ln, ["ltA"], ["ltA"], bias=1.0)
            self.TS("dve", tB[:], tB[:], 0.0, None, ALU.max, None, ["ltB", "ltA"], ["ltB"])
            self.TT("dve", sp8[:], tA[:], tB[:], ALU.add, ["ltA", "ltB"], ["lsp8"])
            self.TS("dve", sp8[:], sp8[:], -8.0, None, ALU.mult, None, ["lsp8"], ["lsp8"])
            xp = self.sb(st, "lxp", [128, 2, XW], F32)
            xc = self.sb(st, "lxc", [128, 2, NTOK], F32)
            xcb = self.sb(st, "lxcb", [128, 2, NTOK], BF16)
            A = self.sb(st, "lA", [128, NTOK], F32)
            G = self.sb(st, "lG", [128, NTOK], F32)
            Tm = self.sb(st, "lTm", [128, NTOK], F32)
            HS = self.sb(st, "lHS", [128, NTOK], F32)
            gg = self.sb(st, "lgg", [128, NTOK], F32)
            yb = self.sb(st, "lyb", [128, NTOK], BF16)
            wa = [self.sb(st, f"lwa{i}", [128, 2, 256], BF16) for i in range(2)]
            wxg = [self.sb(st, f"lwxg{i}", [128, 2, 256], BF16) for i in range(2)]
            pr = [self.ps(st, f"lpr{i}", [128, 512], F32) for i in range(2)]
            pi_ = [self.ps(st, f"lpi{i}", [128, 512], F32) for i in range(2)]
            self.S.add("dve", lambda e: e.memset(xp[:], 0.0), [], ["lxp"])
            it = 0
            wi = 0
            for kb in range(4):
                for cc in range(2):
                    c = 2 * kb + cc
                    self.DMA("sp", xp[:, cc, LAT0:LAT0 + NLAT], self.XR[c * 128:(c + 1) * 128, 0:NLAT],
                             [], ["lxp"], ("lxp", cc))
                    self.DMA("sp", xp[:, cc, CTX0:CTX0 + NCTX], self.XR[c * 128:(c + 1) * 128, NLAT:NTOK],
                             [], ["lxp"], ("lxp", cc))
                    for (o0, x0, n) in ((0, LAT0 - 2, NLAT), (NLAT, CTX0 - 2, NCTX)):
                        dsto = xc[:, cc, o0:o0 + n]
                        self.TS("dve", dsto, xp[:, cc, x0:x0 + n], cols[:, c, 0:1], cols[:, c, 4:5],
                                ALU.mult, ALU.add, ["lxp", "lcols"], [("lxc", cc)])
                        for j in range(1, 4):
                            self.STT(dsto, xp[:, cc, x0 + j:x0 + j + n], cols[:, c, j:j + 1], dsto, ALU.mult, ALU.add,
                                     ["lxp", "lcols", ("lxc", cc)], [("lxc", cc)])
                    self.CP("pool", xcb[:, cc, :], xc[:, cc, :], [("lxc", cc)], [("lxcb", cc)])
                for cc in range(2):
                    c = 2 * kb + cc
                    self.DMA("sp", gg[:], self.GG[c * 128:(c + 1) * 128, :], [], ["lgg"], "lgg")
                    for d in range(2):
                        wb = wi % 2
                        wi += 1
                        self.DMA("pool", wa[wb][:],
                                 self.din["lru_w_a"][idx, d, kb].rearrange("(ic p) j -> p ic j", p=128),
                                 [], [("lwa", wb)], ("lwa", wb))
                        self.DMA("pool", wxg[wb][:],
                                 self.din["lru_w_x"][idx, d, kb].rearrange("(ic p) j -> p ic j", p=128),
                                 [], [("lwxg", wb)], ("lwxg", wb))
                        for bi, (t0, n) in enumerate(blks):
                            pb = it % 2
                            it += 1
                            for ic in range(2):
                                self.MM(pr[pb][:, :n], wa[wb][:, ic, cc * 128:(cc + 1) * 128], xcb[:, ic, t0:t0 + n],
                                        ic == 0, ic == 1, [("lwa", wb), ("lxcb", 0), ("lxcb", 1)], [("lpr", pb)])
                            for ic in range(2):
                                self.MM(pi_[pb][:, :n], wxg[wb][:, ic, cc * 128:(cc + 1) * 128], xcb[:, ic, t0:t0 + n],
                                        ic == 0, ic == 1, [("lwxg", wb), ("lxcb", 0), ("lxcb", 1)], [("lpi", pb)])
                            self.ACT(A[:, t0:t0 + n], pr[pb][:, :n], AF.Sigmoid, [("lpr", pb), "lcols"], ["lA"],
                                     bias=cols[:, c, 5 + d:6 + d])
                            self.ACT(G[:, t0:t0 + n], pi_[pb][:, :n], AF.Sigmoid, [("lpi", pb), "lcols"], ["lG"],
                                     bias=cols[:, c, 7 + d:8 + d])
                        self.ACT(A[:], A[:], AF.Exp, ["lA", "lsp8"], ["lA"], scale=sp8[:, c, d:d + 1])
                        self.TT("pool", Tm[:], A[:], A[:], ALU.mult, ["lA"], ["lTm"])
                        self.ACT(Tm[:], Tm[:], AF.Sqrt, ["lTm"], ["lTm"], bias=1.0, scale=-1.0)
                        self.TT("dve", G[:], G[:], xc[:, cc, :], ALU.mult, ["lG", ("lxc", cc)], ["lG"])
                        self.TT("dve", G[:], G[:], Tm[:], ALU.mult, ["lG", "lTm"], ["lG"])
                        dst = HS if d == 0 else Tm
                        dkey = "lHS" if d == 0 else "lTm"
                        if d == 0:
                            self.S.add("dve", lambda e, dst=dst: e.tensor_tensor_scan(
                                out=dst[:, NLAT:NTOK], data0=A[:, NLAT:NTOK], data1=G[:, NLAT:NTOK], initial=0.0,
                                op0=ALU.mult, op1=ALU.add), ["lA", "lG"], [dkey])
                            self.S.add("dve", lambda e, dst=dst: e.tensor_tensor_scan(
                                out=dst[:, 0:NLAT], data0=A[:, 0:NLAT], data1=G[:, 0:NLAT],
                                initial=dst[:, NTOK - 1:NTOK], op0=ALU.mult, op1=ALU.add), ["lA", "lG", dkey], [dkey])
                        else:
                            self.S.add("dve", lambda e, dst=dst: e.tensor_tensor_scan(
                                out=dst[:, NLAT:NTOK][:, ::-1], data0=A[:, NLAT:NTOK][:, ::-1],
                                data1=G[:, NLAT:NTOK][:, ::-1], initial=0.0,
                                op0=ALU.mult, op1=ALU.add), ["lA", "lG"], [dkey])
                            self.S.add("dve", lambda e, dst=dst: e.tensor_tensor_scan(
                                out=dst[:, 0:NLAT][:, ::-1], data0=A[:, 0:NLAT][:, ::-1], data1=G[:, 0:NLAT][:, ::-1],
                                initial=dst[:, NLAT:NLAT + 1], op0=ALU.mult, op1=ALU.add), ["lA", "lG", dkey], [dkey])
                            self.TT("pool", HS[:], HS[:], Tm[:], ALU.add, ["lHS", "lTm"], ["lHS"])
                    self.TT("dve", yb[:], gg[:], HS[:], ALU.mult, ["lgg", "lHS"], ["lyb"])
                    self.DMA("pool", self.YT[c * 128:(c + 1) * 128, :], yb[:], ["lyb"], [("YT", c)], "lyb")
            S.flush()
        with ExitStack() as st:
            bc = self.load_bc_epi(st, L, 1, 1.0)
            wo = self.sb(st, "lwo", [128, 8, D], BF16)
            for i in range(2):
                self.DMA("pool", wo[:, i * 4:(i + 1) * 4, :],
                         w_out[i * 512:(i + 1) * 512, :].rearrange("(f p) d -> p f d", p=128),
                         [], [("wo", i)], ("wo", i))
            ybk = [self.sb(st, f"lybk{i}", [128, 8, 512], BF16) for i in range(2)]
            eb = self.make_ebufs(st)
            py = [self.ps(st, f"py{i}", [128, D], F32) for i in range(2)]
            ublks = blks if len(tiles) > 32 else blks[:8]
            for bi, (t0, n) in enumerate(ublks):
                b = bi % 2
                self.DMA("sp", ybk[b][:, :, :n], self.YT[:, t0:t0 + n].rearrange("(f p) t -> p f t", p=128),
                         [], [("lybk", b)], ("lybk", b))
                for ti in range(n // 128):
                    t = t0 // 128 + ti
                    par = t % 2
                    for f in range(8):
                        for hh in range(2):
                            self.MM(py[par][:, hh * 512:(hh + 1) * 512], ybk[b][:, f, ti * 128:(ti + 1) * 128],
                                    wo[:, f, hh * 512:(hh + 1) * 512], f == 0, f == 7,
                                    [("lybk", b), ("wo", f // 4)], [("py", par)])
                    self.epilogue(t, py[par][:], [("py", par)], bc, eb)
            S.flush()


_WEIGHTS = ("ada_w", "ada_b", "ln_g", "ln_b", "ffn_w_in", "ffn_w_out", "ret_w_in", "ret_w_out",
            "na_w_qkv", "na_rpb", "na_w_out", "lru_w_in", "lru_conv_w", "lru_conv_b", "lru_w_a",
            "lru_b_a", "lru_w_x", "lru_b_x", "lru_lam", "lru_w_out")


def const_inputs():
    p = np.arange(128)
    freqs = 10000.0 ** (-(2.0 * (p % 64)) / 128.0)
    ang = np.arange(64, dtype=np.float64)[None, :] * freqs[:, None]
    cos = np.cos(ang)
    sin = np.sin(ang) * np.where(p < 64, -1.0, 1.0)[:, None]
    dec = np.zeros((16, 128), np.float64)
    i = np.arange(128, dtype=np.float64)
    for dr in range(2):
        for h in range(4):
            gh = h if dr == 0 else 3 - h
            lg = math.log1p(-(2.0 ** (-5 - gh)))
            e = (i + 1.0) if dr == 0 else (128.0 - i)
            dec[dr * 8 + h] = np.exp(lg * e)
            dec[dr * 8 + 4 + h] = np.exp(-lg * e) / 16.0
    j = np.arange(128)[:, None]
    ii = np.arange(128)[None, :]
    mask = np.stack([(j <= ii), (j > ii)], axis=1).astype(np.float32)
    kc = np.arange(64)[:, None]
    qc = np.arange(64)[None, :]
    cs = np.clip(qc - 8, 0, 48)
    ok = (kc >= cs) & (kc < cs + 16)
    namask = np.where(np.concatenate([ok, ok], axis=0), 0.0, -1e30).astype(np.float32)
    j2 = np.zeros((128, 128), np.float32)
    for par in range(2):
        for a in range(64):
            j2[par * 64 + a, par * 64 + 63 - a] = 1.0
    return {"k_ident": np.eye(128, dtype=np.float32), "k_namask": namask, "k_j2": j2,
            "k_cos": cos.astype(np.float32), "k_sin": sin.astype(np.float32),
            "k_dec": np.ascontiguousarray(np.broadcast_to(dec.astype(np.float32)[None], (128, 16, 128))),
            "k_mask": np.ascontiguousarray(mask)}


def make_in_map(inputs, b, consts):
    m = {k: np.ascontiguousarray(inputs[k], dtype=np.float32) for k in _WEIGHTS}
    m["x"] = np.ascontiguousarray(inputs["x"][b], dtype=np.float32)
    m["ctx"] = np.ascontiguousarray(inputs["ctx"][b], dtype=np.float32)
    cl = np.asarray(inputs["c"][b], dtype=np.float32).reshape(8, 128).T
    cc = np.asarray(inputs["c_ctx"], dtype=np.float32).reshape(8, 128).T
    m["c2"] = np.ascontiguousarray(np.stack([cl, cc], axis=-1))
    colv = [np.asarray(inputs["lru_conv_w"], np.float32)[0, j] for j in range(4)]
    colv.append(np.asarray(inputs["lru_conv_b"], np.float32)[0])
    for nm in ("lru_b_a", "lru_b_x", "lru_lam"):
        colv += [np.asarray(inputs[nm], np.float32)[0, 0], np.asarray(inputs[nm], np.float32)[0, 1]]
    m["lru_cols"] = np.ascontiguousarray(np.stack(colv, 0).reshape(11, 8, 128).transpose(2, 1, 0))
    rpb = np.asarray(inputs["na_rpb"], dtype=np.float32)[0]
    rp = np.zeros((16, 15, 127), np.float32)
    rp[:, :, 48:79] = rpb[:, :, ::-1]
    m["rpb_pad"] = rp
    m.update(consts)
    return m


def kernel(**inputs):
    prog = Prog()
    nc = prog.build()
    consts = const_inputs()
    in_maps = [make_in_map(inputs, b, consts) for b in range(N_CORES)]
    res = run_bass_kernel_spmd(nc, in_maps, core_ids=list(range(N_CORES)))
    return np.stack([np.asarray(r["out"], dtype=np.float32) for r in res.results], axis=0)
```

```python
import math
from contextlib import ExitStack

import numpy as np
import concourse.bass as bass
import concourse.mybir as mybir
from concourse.bass_utils import run_bass_kernel_spmd

F32 = mybir.dt.float32
BF16 = mybir.dt.bfloat16
AF = mybir.ActivationFunctionType
ALU = mybir.AluOpType

D = 1024
NLAT = 4096
NCTX = 256
NTOK = NLAT + NCTX
NT = NTOK // 128
DFF = 2816
DEPTH = 4
ALPHA = (2 * DEPTH) ** 0.25
EPS = 1e-5
N_CORES = 8


class Op:
    __slots__ = ("eng", "fn", "reads", "writes", "slot", "deps", "ticket")

    def __init__(self, eng, fn, reads, writes, slot):
        self.eng = eng
        self.fn = fn
        self.reads = reads
        self.writes = writes
        self.slot = slot
        self.deps = ()
        self.ticket = None


class Sched:
    ENGS = ("pe", "act", "dve", "pool", "sp")

    def __init__(self, nc, stack, nsem=100):
        self.nc = nc
        self.sems = [stack.enter_context(nc.semaphore(f"s{i}")) for i in range(nsem)]
        self.eng_sem = {e: i for i, e in enumerate(("pe", "act", "dve", "pool"))}
        self.bar_sem = 4
        self.bar_count = 0
        self.first_slot_sem = 5
        self.cur = []
        self.nops = 0
        self.counts = {e: 0 for e in self.ENGS}
        self.slot_sem = {}
        self.slot_cnt = {}
        self.next_sem = nsem - 1

    def add(self, eng, fn, reads=(), writes=(), slot=None):
        self.cur.append(Op(eng, fn, tuple(reads), tuple(writes), slot))

    def dma(self, eng, out, in_, reads, writes, slot):
        self.add(eng, lambda e: e.dma_start(out=out, in_=in_), reads, writes, slot)

    def flush(self):
        ops = self.cur
        self.cur = []
        n = len(ops)
        if n == 0:
            return
        self.nops += n
        last_writer = {}
        readers = {}
        has_dep = [False] * n
        for i, op in enumerate(ops):
            deps = set()
            for k in op.reads:
                w = last_writer.get(k)
                if w is not None:
                    deps.add(w)
            for k in op.writes:
                w = last_writer.get(k)
                if w is not None:
                    deps.add(w)
                rd = readers.get(k)
                if rd:
                    deps.update(rd[0].values())
                    deps.update(rd[1])
            deps.discard(i)
            if op.eng == "pe" and op.slot is None:
                deps = {d for d in deps if not (ops[d].eng == "pe" and ops[d].slot is None)}
            op.deps = deps
            for d in deps:
                has_dep[d] = True
            for k in op.reads:
                rd = readers.get(k)
                if rd is None:
                    rd = readers[k] = ({}, [])
                if op.slot is None:
                    rd[0][op.eng] = i
                else:
                    rd[1].append(i)
            for k in op.writes:
                last_writer[k] = i
                readers[k] = ({}, [])
        last_compute = {}
        for i, op in enumerate(ops):
            if op.slot is None:
                last_compute[op.eng] = i
        for e, i in last_compute.items():
            has_dep[i] = True
        counts = self.counts
        slots_used = {}
        local_slot = {}
        for i, op in enumerate(ops):
            if op.slot is not None:
                if op.slot not in local_slot:
                    sem = self.first_slot_sem + len(local_slot)
                    assert sem < len(self.sems), "out of DMA slot semaphores"
                    local_slot[op.slot] = sem
                sem = local_slot[op.slot]
                self.slot_cnt[sem] = self.slot_cnt.get(sem, 0) + 16
                slots_used[sem] = self.slot_cnt[sem]
                op.ticket = (sem, self.slot_cnt[sem], 16)
            elif has_dep[i]:
                assert op.eng in self.eng_sem, f"compute op on {op.eng}"
                counts[op.eng] += 1
                op.ticket = (self.eng_sem[op.eng], counts[op.eng], 1)
        waited = {e: {} for e in self.ENGS}
        streams = {e: [] for e in self.ENGS}
        for i, op in enumerate(ops):
            need = {}
            for d in op.deps:
                sem, val, _ = ops[d].ticket
                if need.get(sem, 0) < val:
                    need[sem] = val
            waits = []
            wd = waited[op.eng]
            for sem, val in need.items():
                if wd.get(sem, 0) >= val:
                    continue
                wd[sem] = val
                waits.append((sem, val))
            streams[op.eng].append((waits, op))
        self.bar_count += 1
        bar_val = self.bar_count
        sems = self.sems
        final_waits = [(self.eng_sem[e], counts[e]) for e in last_compute]
        final_waits += list(slots_used.items())

        def make_body(ename):
            def body(eng):
                for waits, op in streams[ename]:
                    for sem, val in waits:
                        eng.wait_ge(sems[sem], val)
                    ins = op.fn(eng)
                    if op.ticket is not None:
                        ins.then_inc(sems[op.ticket[0]], op.ticket[2])
                if ename == "sp":
                    for sem, val in final_waits:
                        eng.wait_ge(sems[sem], val)
                    eng.sem_inc(sems[self.bar_sem], 1)
                else:
                    eng.wait_ge(sems[self.bar_sem], bar_val)
            return body

        with self.nc.Block() as block:
            block.sync(make_body("sp"))
            block.tensor(make_body("pe"))
            block.scalar(make_body("act"))
            block.vector(make_body("dve"))
            block.gpsimd(make_body("pool"))


class Prog:
    def __init__(self, nsub=12, dbg=False):
        self.nsub = nsub
        self.dbg = dbg
        self.nc = nc = bass.Bass("TRN2", target_bir_lowering=False)
        self.din = {}
        self.uid = 0

        def din(name, shape, dt=F32):
            self.din[name] = nc.dram_tensor(name, list(shape), dt, kind="ExternalInput").ap()

        din("x", [NLAT, D]); din("ctx", [NCTX, D]); din("c2", [128, 8, 2])
        din("ada_w", [4, D, 9 * D]); din("ada_b", [4, 9 * D])
        din("ln_g", [4, 3, D]); din("ln_b", [4, 3, D])
        din("ffn_w_in", [4, 2, D, 2 * DFF]); din("ffn_w_out", [4, 2, DFF, D])
        din("ret_w_in", [2, D, 6144]); din("ret_w_out", [2, 2048, D])
        din("na_w_qkv", [1, D, 3072]); din("na_rpb", [1, 16, 15, 31]); din("na_w_out", [1, D, D])
        din("lru_w_in", [1, D, 2048]); din("lru_conv_w", [1, 4, D]); din("lru_conv_b", [1, D])
        din("lru_w_a", [1, 2, 4, 256, 256]); din("lru_b_a", [1, 2, D])
        din("lru_w_x", [1, 2, 4, 256, 256]); din("lru_b_x", [1, 2, D])
        din("lru_lam", [1, 2, D]); din("lru_w_out", [1, D, D])
        din("k_ident", [128, 128]); din("k_cos", [128, 64]); din("k_sin", [128, 64])
        din("k_dec", [128, 16, 128]); din("k_mask", [128, 2, 128])
        din("k_j2", [128, 128]); din("k_namask", [128, 64]); din("rpb_pad", [16, 15, 127]); din("lru_cols", [128, 8, 11])
        self.out = nc.dram_tensor("out", [NLAT, D], F32, kind="ExternalOutput").ap()
        if dbg:
            self.outc = nc.dram_tensor("outc", [NCTX, D], F32, kind="ExternalOutput").ap()
        self.Hd = nc.dram_tensor("Hd", [NTOK, D], F32, kind="Internal").ap()
        self.MODS = nc.dram_tensor("MODS", [4, 2, 9 * D], F32, kind="Internal").ap()
        self.GT = nc.dram_tensor("GT", [DFF, NTOK], BF16, kind="Internal").ap()
        self.QT = nc.dram_tensor("QT", [2, D, NTOK], BF16, kind="Internal").ap()
        self.KT = nc.dram_tensor("KT", [2, D, NTOK], BF16, kind="Internal").ap()
        self.Vd = nc.dram_tensor("Vd", [NTOK, 2048], BF16, kind="Internal").ap()
        self.SGd = nc.dram_tensor("SGd", [NTOK, 2048], F32, kind="Internal").ap()
        self.Od = nc.dram_tensor("Od", [2, NTOK, 2048], F32, kind="Internal").ap()
        self.XR = nc.dram_tensor("XR", [D, NTOK], F32, kind="Internal").ap()
        self.GG = nc.dram_tensor("GG", [D, NTOK], F32, kind="Internal").ap()
        self.YT = nc.dram_tensor("YT", [D, NTOK], BF16, kind="Internal").ap()

    def sb(self, st, name, shape, dt):
        self.uid += 1
        return st.enter_context(self.nc.sbuf_tensor(f"{name}_{self.uid}", list(shape), dt))

    def ps(self, st, name, shape, dt):
        self.uid += 1
        return st.enter_context(self.nc.psum_tensor(f"{name}_{self.uid}", list(shape), dt))

    def MM(self, out, lhsT, rhs, start, stop, reads, writes):
        self.S.add("pe", lambda e: e.matmul(out, lhsT, rhs, start=start, stop=stop), reads, writes)

    def TR(self, out, in_, reads, writes):
        np_ = in_.shape[0]
        idap = self.ident[:np_, :np_]
        self.S.add("pe", lambda e: e.transpose(out, in_, idap), reads, writes)

    def TT(self, eng, out, in0, in1, op, reads, writes):
        self.S.add(eng, lambda e: e.tensor_tensor(out=out, in0=in0, in1=in1, op=op), reads, writes)

    def TS(self, eng, out, in0, s1, s2, op0, op1, reads, writes):
        if s2 is None:
            self.S.add(eng, lambda e: e.tensor_scalar(out=out, in0=in0, scalar1=s1, scalar2=None, op0=op0),
                       reads, writes)
        else:
            self.S.add(eng, lambda e: e.tensor_scalar(out=out, in0=in0, scalar1=s1, scalar2=s2, op0=op0, op1=op1),
                       reads, writes)

    def STT(self, out, in0, scalar, in1, op0, op1, reads, writes):
        self.S.add("dve", lambda e: e.scalar_tensor_tensor(out=out, in0=in0, scalar=scalar, in1=in1,
                                                           op0=op0, op1=op1), reads, writes)

    def ACT(self, out, in_, func, reads, writes, bias=None, scale=None):
        kw = {}
        if bias is not None:
            kw["bias"] = bias
        if scale is not None:
            kw["scale"] = scale
        self.S.add("act", lambda e: e.activation(out=out, in_=in_, func=func, **kw), reads, writes)

    def CP(self, eng, out, in_, reads, writes):
        if eng == "act":
            self.S.add("act", lambda e: e.copy(out=out, in_=in_), reads, writes)
        else:
            self.S.add(eng, lambda e: e.tensor_copy(out=out, in_=in_), reads, writes)

    def DMA(self, eng, out, in_, reads, writes, slot):
        self.S.dma(eng, out, in_, reads, writes, slot)

    def h_in(self, t):
        if self.sidx == 0:
            if t < 32:
                return self.din["x"][t * 128:(t + 1) * 128, :], []
            return self.din["ctx"][(t - 32) * 128:(t - 31) * 128, :], []
        return self.Hd[t * 128:(t + 1) * 128, :], [("H", t)]

    def h_out(self, t):
        if self.sidx == self.nsub - 1:
            if t < 32:
                return self.out[t * 128:(t + 1) * 128, :]
            if self.dbg:
                return self.outc[(t - 32) * 128:(t - 31) * 128, :]
        return self.Hd[t * 128:(t + 1) * 128, :]

    def build(self):
        nc = self.nc
        with ExitStack() as gst:
            self.S = Sched(nc, gst)
            self.ident = self.sb(gst, "ident", [128, 128], BF16)
            self.DMA("pool", self.ident[:], self.din["k_ident"][:, :], [], ["ident"], "ident")
            self.s2 = self.sb(gst, "s2", [128, 8, 2], BF16)
            self.phase_mod()
            self.sidx = 0
            for L in range(DEPTH):
                last = L == DEPTH - 1
                for sub in range(3):
                    if self.sidx >= self.nsub:
                        break
                    tiles = list(range(32)) if (last and sub == 2 and not self.dbg) else list(range(NT))
                    if sub == 0:
                        self.ffn_sublayer(L, 0, tiles)
                    elif sub == 2:
                        self.ffn_sublayer(L, 1, tiles)
                    else:
                        self.mixer_sublayer(L, tiles)
                    self.sidx += 1
        return nc

    def mod_bufs(self, st):
        return dict(
            wa=[self.sb(st, f"wa{i}", [128, 8, 512], BF16) for i in range(2)],
            bt=[self.sb(st, f"bt{i}", [2, 512], F32) for i in range(2)],
            mo=[self.sb(st, f"mo{i}", [2, 512], F32) for i in range(2)],
            pm=[self.ps(st, f"pm{i}", [2, 512], F32) for i in range(2)])

    def mod_item(self, mb, L, j):
        b = j % 2
        wa, bt, mo, pm = mb["wa"][b], mb["bt"][b], mb["mo"][b], mb["pm"][b]
        s2 = self.s2
        cs = slice(j * 512, (j + 1) * 512)
        self.DMA("pool", wa[:], self.din["ada_w"][L, :, cs].rearrange("(k p) n -> p k n", p=128),
                 [], [("wa", b)], ("wa", b))
        self.DMA("sp", bt[:], self.din["ada_b"][L:L + 1, cs].partition_broadcast(2), [], [("bt", b)], ("bt", b))
        for k in range(8):
            self.MM(pm[:], s2[:, k, :], wa[:, k, :], k == 0, k == 7, ["s2", ("wa", b)], [("pm", b)])
        self.TT("dve", mo[:], pm[:], bt[:], ALU.add, [("pm", b), ("bt", b)], [("mo", b)])
        self.DMA("pool", self.MODS[L, :, cs], mo[:], [("mo", b)], [("MODS", L, j)], ("mo", b))

    def phase_mod(self):
        S = self.S
        with ExitStack() as st:
            c2f = self.sb(st, "c2f", [128, 8, 2], F32)
            self.DMA("sp", c2f[:], self.din["c2"][:, :, :], [], ["c2f"], "c2f")
            self.ACT(self.s2[:], c2f[:], AF.Silu, ["c2f"], ["s2"])
            mb = self.mod_bufs(st)
            for L in range(DEPTH):
                for j in range(18):
                    self.mod_item(mb, L, j)
            S.flush()

    def _bc_mod(self, st, bc, L, nm, mi, side):
        t = self.sb(st, f"bc{nm}{side}", [128, D], F32)
        bc[nm, side] = t
        self.DMA("sp", t[:], self.MODS[L, side:side + 1, mi * D:(mi + 1) * D].partition_broadcast(128),
                 [], [("bc", nm, side)], ("bc", nm, side))
        return t

    def load_bc_prep(self, st, L, sub):
        bc = {}
        for side in (0, 1):
            a = self._bc_mod(st, bc, L, "A", 3 * sub + 1, side)
            self._bc_mod(st, bc, L, "B", 3 * sub, side)
            self.TS("dve", a[:], a[:], 1.0, None, ALU.add, None, [("bc", "A", side)], [("bc", "A", side)])
        return bc

    def load_bc_epi(self, st, L, sub, res):
        bc = {}
        for side in (0, 1):
            g = self._bc_mod(st, bc, L, "G", 3 * sub + 2, side)
            if res != 1.0:
                self.TS("dve", g[:], g[:], float(res), None, ALU.mult, None, [("bc", "G", side)], [("bc", "G", side)])
        for nm, src in (("lg", "ln_g"), ("lb", "ln_b")):
            t = self.sb(st, f"bc{nm}", [128, D], F32)
            bc[nm] = t
            self.DMA("sp", t[:], self.din[src][L, sub:sub + 1, :].partition_broadcast(128), [], [("bc", nm)], ("bc", nm))
        return bc

    def prep_alloc(self, st, bc, uT):
        hl = [self.sb(st, f"hl{i}", [128, D], F32) for i in range(2)]
        ut = [self.sb(st, f"ut{i}", [128, D], F32) for i in range(2)]
        ub = [self.sb(st, f"ub{i}", [128, D], BF16) for i in range(2)]
        ptr = [self.ps(st, f"ptr{i}", [128, D], BF16) for i in range(2)]

        def prep_tile(t):
            par = t % 2
            side = 1 if t >= 32 else 0
            src, rk = self.h_in(t)
            self.DMA("sp", hl[par][:], src, rk, [("hl", par)], ("hl", par))
            self.TT("dve", ut[par][:], hl[par][:], bc["A", side][:], ALU.mult,
                    [("hl", par), ("bc", "A", side)], [("ut", par)])
            self.TT("pool", ub[par][:], ut[par][:], bc["B", side][:], ALU.add,
                    [("ut", par), ("bc", "B", side)], [("ub", par)])
            for k in range(8):
                self.TR(ptr[par][:, k * 128:(k + 1) * 128], ub[par][:, k * 128:(k + 1) * 128],
                        [("ub", par), "ident"], [("ptr", par)])
            self.CP("act", uT[:, :, t * 128:(t + 1) * 128], ptr[par][:].rearrange("p (k t) -> p k t", k=8),
                    [("ptr", par)], [("uT", t)])
        return prep_tile

    def prep(self, st, bc, uT, tiles):
        prep_tile = self.prep_alloc(st, bc, uT)
        for t in tiles:
            prep_tile(t)

    def block_prepper(self, prep_tile, blks):
        state = {"n": 0}

        def ensure(nb):
            while state["n"] < min(nb, len(blks)):
                t0, n = blks[state["n"]]
                for t in range(t0 // 128, (t0 + n) // 128):
                    prep_tile(t)
                state["n"] += 1
        return ensure

    def make_ebufs(self, st):
        eb = {}
        for nm in ("eh", "t1", "xx", "hn"):
            eb[nm] = [self.sb(st, f"e{nm}{i}", [128, D], F32) for i in range(2)]
        eb["st6"] = [self.sb(st, f"est6{i}", [128, 2, 6], F32) for i in range(2)]
        eb["mv"] = [self.sb(st, f"emv{i}", [128, 2], F32) for i in range(2)]
        eb["rs"] = [self.sb(st, f"ers{i}", [128, 1], F32) for i in range(2)]
        return eb

    def epilogue(self, t, yap, ykeys, bc, eb):
        par = t % 2
        side = 1 if t >= 32 else 0
        eh, t1, xx, hn = eb["eh"][par], eb["t1"][par], eb["xx"][par], eb["hn"][par]
        st6, mv, rs = eb["st6"][par], eb["mv"][par], eb["rs"][par]
        src, rk = self.h_in(t)
        self.DMA("sp", eh[:], src, rk, [("eh", par)], ("eh", par))
        self.TT("dve", t1[:], yap, bc["G", side][:], ALU.mult, list(ykeys) + [("bc", "G", side)], [("t1", par)])
        self.STT(xx[:], eh[:], float(ALPHA), t1[:], ALU.mult, ALU.add, [("eh", par), ("t1", par)], [("xx", par)])
        for j in range(2):
            self.S.add("dve", lambda e, j=j: e.bn_stats(out=st6[:, j, :], in_=xx[:, j * 512:(j + 1) * 512]),
                       [("xx", par)], [("st6", par)])
        self.S.add("dve", lambda e: e.bn_aggr(out=mv[:], in_=st6[:].rearrange("p a b -> p (a b)")),
                   [("st6", par)], [("mv", par)])
        self.TS("dve", rs[:], mv[:, 1:2], EPS, None, ALU.add, None, [("mv", par)], [("rs", par)])
        self.ACT(rs[:], rs[:], AF.Sqrt, [("rs", par)], [("rs", par)])
        self.S.add("dve", lambda e: e.reciprocal(out=rs[:], in_=rs[:]), [("rs", par)], [("rs", par)])
        self.TS("dve", t1[:], xx[:], mv[:, 0:1], rs[:], ALU.subtract, ALU.mult,
                [("xx", par), ("mv", par), ("rs", par)], [("t1", par)])
        self.TT("pool", hn[:], t1[:], bc["lg"][:], ALU.mult, [("t1", par), ("bc", "lg")], [("hn", par)])
        self.TT("pool", hn[:], hn[:], bc["lb"][:], ALU.add, [("hn", par), ("bc", "lb")], [("hn", par)])
        self.DMA("pool", self.h_out(t), hn[:], [("hn", par)], [("H", t)], ("hn", par))

    def ffn_sublayer(self, L, half, tiles):
        S = self.S
        sub = 0 if half == 0 else 2
        nt = len(tiles)
        blks = [(i * 512, 512) for i in range(8)]
        if nt > 32:
            blks.append((NLAT, NCTX))
        w_in = self.din["ffn_w_in"][L, half]
        w_out = self.din["ffn_w_out"][L, half]
        with ExitStack() as sst:
            wo = self.sb(sst, "wo", [128, 22, D], BF16)
            wparts = [(0, 6), (6, 12), (12, 18), (18, 22)]
            with ExitStack() as st:
                bc = self.load_bc_prep(st, L, sub)
                uT = self.sb(st, "uT", [128, 8, NTOK], BF16)
                ensure = self.block_prepper(self.prep_alloc(st, bc, uT), blks)
                wg = [self.sb(st, f"wg{i}", [128, 8, 256], BF16) for i in range(2)]
                wu = [self.sb(st, f"wu{i}", [128, 8, 256], BF16) for i in range(2)]
                sg = [self.sb(st, f"sg{i}", [128, 512], F32) for i in range(2)]
                gs = [self.sb(st, f"gs{i}", [128, 512], BF16) for i in range(2)]
                pg = [self.ps(st, f"pg{i}", [128, 512], F32) for i in range(2)]
                pu = [self.ps(st, f"pu{i}", [128, 512], F32) for i in range(2)]
                it = 0

                def load_w(g):
                    b = g % 2
                    self.DMA("pool", wg[b][:], w_in[:, g * 256:(g + 1) * 256].rearrange("(k p) n -> p k n", p=128),
                             [], [("wg", b)], ("wg", b))
                    self.DMA("pool", wu[b][:],
                             w_in[:, DFF + g * 256:DFF + (g + 1) * 256].rearrange("(k p) n -> p k n", p=128),
                             [], [("wu", b)], ("wu", b))

                load_w(0)
                for g in range(11):
                    b = g % 2
                    if g + 1 < 11:
                        load_w(g + 1)
                    for bi, (t0, n) in enumerate(blks):
                        if g == 0:
                            ensure(bi + 2)
                        ukeys = [("uT", t) for t in range(t0 // 128, (t0 + n) // 128)]
                        for fc in range(2):
                            pb = it % 2
                            it += 1
                            for k in range(8):
                                self.MM(pg[pb][:, :n], wg[b][:, k, fc * 128:(fc + 1) * 128], uT[:, k, t0:t0 + n],
                                        k == 0, k == 7, [("wg", b)] + ukeys, [("pg", pb)])
                            for k in range(8):
                                self.MM(pu[pb][:, :n], wu[b][:, k, fc * 128:(fc + 1) * 128], uT[:, k, t0:t0 + n],
                                        k == 0, k == 7, [("wu", b)] + ukeys, [("pu", pb)])
                            self.ACT(sg[pb][:, :n], pg[pb][:, :n], AF.Silu, [("pg", pb)], [("sg", pb)])
                            self.TT("dve", gs[pb][:, :n], pu[pb][:, :n], sg[pb][:, :n], ALU.mult,
                                    [("pu", pb), ("sg", pb)], [("gs", pb)])
                            r0 = g * 256 + fc * 128
                            self.DMA("pool", self.GT[r0:r0 + 128, t0:t0 + n], gs[pb][:, :n],
                                     [("gs", pb)], [("GT", g, fc, bi)], ("gs", pb))
                S.flush()
            with ExitStack() as st:
                bc = self.load_bc_epi(st, L, sub, 0.5)
                mod_todo = []
                mb = self.mod_bufs(st) if mod_todo else None
                for pi, (f0, f1) in enumerate(wparts):
                    self.DMA("pool", wo[:, f0:f1, :],
                             w_out[f0 * 128:f1 * 128, :].rearrange("(f p) d -> p f d", p=128),
                             [], [("wo", pi)], ("wo", pi))
                gb = [self.sb(st, f"gb{i}", [128, 22, 512], BF16) for i in range(2)]
                eb = self.make_ebufs(st)
                py = [self.ps(st, f"py{i}", [128, D], F32) for i in range(2)]
                for bi, (t0, n) in enumerate(blks):
                    b = bi % 2
                    self.DMA("sp", gb[b][:, :, :n], self.GT[:, t0:t0 + n].rearrange("(f p) t -> p f t", p=128),
                             [], [("gb", b)], ("gb", b))
                    for ti in range(n // 128):
                        t = t0 // 128 + ti
                        par = t % 2
                        for f in range(22):
                            pi = [i for i, (f0, f1) in enumerate(wparts) if f0 <= f < f1][0]
                            for hh in range(2):
                                self.MM(py[par][:, hh * 512:(hh + 1) * 512], gb[b][:, f, ti * 128:(ti + 1) * 128],
                                        wo[:, f, hh * 512:(hh + 1) * 512], f == 0, f == 21,
                                        [("gb", b), ("wo", pi)], [("py", par)])
                        if mod_todo:
                            self.mod_item(mb, L + 1, mod_todo.pop(0))
                        self.epilogue(t, py[par][:], [("py", par)], bc, eb)
                S.flush()

    def mixer_sublayer(self, L, tiles):
        kind, idx = L % 3, L // 3
        if kind == 0:
            self.ret_sublayer(L, idx, tiles)
        elif kind == 1:
            self.na_sublayer(L, idx, tiles)
        else:
            self.lru_sublayer(L, idx, tiles)

    def ret_sublayer(self, L, idx, tiles):
        S = self.S
        w_in = self.din["ret_w_in"][idx]
        w_out = self.din["ret_w_out"][idx]
        blks = [(i * 512, 512) for i in range(8)] + [(NLAT, NCTX)]
        alltiles = list(range(NT))
        with ExitStack() as ust:
          uT = self.sb(ust, "uT", [128, 8, NTOK], BF16)
          with ExitStack() as st:
            bc = self.load_bc_prep(st, L, 1)
            ensure = self.block_prepper(self.prep_alloc(st, bc, uT), blks)
            cos = self.sb(st, "cos", [128, 64], F32)
            sin = self.sb(st, "sin", [128, 64], F32)
            dec = self.sb(st, "dec", [128, 16, 128], F32)
            self.DMA("sp", cos[:], self.din["k_cos"][:, :], [], ["cos"], "cos")
            self.DMA("sp", sin[:], self.din["k_sin"][:, :], [], ["sin"], "sin")
            self.DMA("sp", dec[:], self.din["k_dec"][:, :, :], [], ["dec"], "dec")
            wq = [self.sb(st, f"wq{i}", [128, 8, 128], BF16) for i in range(2)]
            wqs = [self.sb(st, f"wqs{i}", [128, 8, 128], BF16) for i in range(2)]
            t1 = [self.sb(st, f"rt1{i}", [128, 512], F32) for i in range(2)]
            t2 = [self.sb(st, f"rt2{i}", [128, 512], F32) for i in range(2)]
            rot = [self.sb(st, f"rot{i}", [128, 512], F32) for i in range(2)]
            qst = [self.sb(st, f"qst{i}", [128, 512], BF16) for i in range(4)]
            pq = [self.ps(st, f"pq{i}", [128, 512], F32) for i in range(2)]
            pqs = [self.ps(st, f"pqs{i}", [128, 512], F32) for i in range(2)]

            def load_wq(c16):
                b = c16 % 2
                col0 = (1024 if c16 >= 8 else 0) + (c16 % 8) * 128
                self.DMA("pool", wq[b][:], w_in[:, col0:col0 + 128].rearrange("(k p) n -> p k n", p=128),
                         [], [("wq", b)], ("wq", b))
                self.CP("act", wqs[b][:, :, 0:64], wq[b][:, :, 64:128], [("wq", b)], [("wqs", b)])
                self.CP("act", wqs[b][:, :, 64:128], wq[b][:, :, 0:64], [("wq", b)], [("wqs", b)])

            it = 0
            load_wq(0)
            for c16 in range(16):
                b = c16 % 2
                isk = c16 >= 8
                c = c16 % 8
                h = c // 2
                if c16 + 1 < 16:
                    load_wq(c16 + 1)
                dst = self.KT if isk else self.QT
                for bi, (t0, n) in enumerate(blks):
                    if c16 == 0:
                        ensure(bi + 2)
                    lat = t0 < NLAT
                    pb = it % 2
                    it += 1
                    ukeys = [("uT", t) for t in range(t0 // 128, (t0 + n) // 128)]
                    for k in range(8):
                        self.MM(pq[pb][:, :n], wq[b][:, k, :], uT[:, k, t0:t0 + n], k == 0, k == 7,
                                [("wq", b)] + ukeys, [("pq", pb)])
                    if lat:
                        for k in range(8):
                            self.MM(pqs[pb][:, :n], wqs[b][:, k, :], uT[:, k, t0:t0 + n], k == 0, k == 7,
                                    [("wqs", b)] + ukeys, [("pqs", pb)])
                        nr = n // 64
                        r0 = t0 // 64
                        if c % 2 == 0:
                            cosap = cos[:, r0:r0 + nr].unsqueeze(2).to_broadcast([128, nr, 64])
                            sinap = sin[:, r0:r0 + nr].unsqueeze(2).to_broadcast([128, nr, 64])
                        else:
                            cosap = cos[:, :].unsqueeze(1).to_broadcast([128, nr, 64])
                            sinap = sin[:, :].unsqueeze(1).to_broadcast([128, nr, 64])
                        v3 = lambda ap: ap.rearrange("p (r c) -> p r c", c=64)
                        self.TT("dve", v3(t1[pb][:, :n]), v3(pq[pb][:, :n]), cosap, ALU.mult,
                                [("pq", pb), "cos"], [("t1", pb)])
                        self.TT("dve", v3(t2[pb][:, :n]), v3(pqs[pb][:, :n]), sinap, ALU.mult,
                                [("pqs", pb), "sin"], [("t2", pb)])
                        self.TT("pool", rot[pb][:, :n], t1[pb][:, :n], t2[pb][:, :n], ALU.add,
                                [("t1", pb), ("t2", pb)], [("rot", pb)])
                    else:
                        self.CP("act", rot[pb][:, :n], pq[pb][:, :n], [("pq", pb)], [("rot", pb)])
                    for dr in range(2):
                        di = dr * 8 + (4 if isk else 0) + h
                        nch = n // 128
                        decap = dec[:, di, :].unsqueeze(1).to_broadcast([128, nch, 128])
                        v4 = lambda ap: ap.rearrange("p (a b) -> p a b", b=128)
                        qb = dr * 2 + pb
                        self.TT("dve" if dr == 0 else "pool", v4(qst[qb][:, :n]), v4(rot[pb][:, :n]), decap, ALU.mult,
                                [("rot", pb), "dec"], [("qst", qb)])
                        self.DMA("pool", dst[dr, c * 128:(c + 1) * 128, t0:t0 + n], qst[qb][:, :n],
                                 [("qst", qb)], [("QK", c16, bi, dr)], ("qst", qb))
            S.flush()
          with ExitStack() as st:
            wv = self.sb(st, "wv", [128, 8, 2048], BF16)
            wgg = self.sb(st, "wgg", [128, 8, 2048], BF16)
            for i in range(4):
                self.DMA("pool", wv[:, :, i * 512:(i + 1) * 512],
                         w_in[:, 2048 + i * 512:2048 + (i + 1) * 512].rearrange("(k p) n -> p k n", p=128),
                         [], [("wv", i)], ("wv", i))
            for i in range(4):
                self.DMA("pool", wgg[:, :, i * 512:(i + 1) * 512],
                         w_in[:, 4096 + i * 512:4096 + (i + 1) * 512].rearrange("(k p) n -> p k n", p=128),
                         [], [("wgg", i)], ("wgg", i))
            vst = [self.sb(st, f"vst{i}", [128, 2048], BF16) for i in range(2)]
            sgst = [self.sb(st, f"sgst{i}", [128, 2048], F32) for i in range(2)]
            pv = [self.ps(st, f"pv{i}", [128, 512], F32) for i in range(2)]
            it = 0
            for t in alltiles:
                par = t % 2
                for g4 in range(4):
                    pb = it % 2
                    it += 1
                    for k in range(8):
                        self.MM(pv[pb][:], uT[:, k, t * 128:(t + 1) * 128], wv[:, k, g4 * 512:(g4 + 1) * 512],
                                k == 0, k == 7, [("uT", t), ("wv", g4)], [("pv", pb)])
                    self.CP("act", vst[par][:, g4 * 512:(g4 + 1) * 512], pv[pb][:], [("pv", pb)], [("vst", par)])
                self.DMA("pool", self.Vd[t * 128:(t + 1) * 128, :], vst[par][:], [("vst", par)], [("Vd", t)], ("vst", par))
                for g4 in range(4):
                    pb = it % 2
                    it += 1
                    for k in range(8):
                        self.MM(pv[pb][:], uT[:, k, t * 128:(t + 1) * 128], wgg[:, k, g4 * 512:(g4 + 1) * 512],
                                k == 0, k == 7, [("uT", t), ("wgg", g4)], [("pv", pb)])
                    self.ACT(sgst[par][:, g4 * 512:(g4 + 1) * 512], pv[pb][:], AF.Silu, [("pv", pb)], [("sgst", par)])
                self.DMA("pool", self.SGd[t * 128:(t + 1) * 128, :], sgst[par][:], [("sgst", par)], [("SGd", t)],
                         ("sgst", par))
            S.flush()
        with ExitStack() as st:
            msk = self.sb(st, "msk", [128, 2, 128], F32)
            self.DMA("sp", msk[:], self.din["k_mask"][:, :, :], [], ["msk"], "msk")
            T = self.sb(st, "Tst", [128, 16, 512], F32)
            Sb = self.sb(st, "Sbf", [128, 16, 512], BF16)
            self.S.add("dve", lambda e: e.memset(T[:], 0.0), [], [("T", i) for i in range(16)])
            self.S.add("dve", lambda e: e.memset(Sb[:], 0.0), [], [("Sb", i) for i in range(16)])
            qt = [self.sb(st, f"qt{i}", [128, 8, 128], BF16) for i in range(4)]
            kt = [self.sb(st, f"kt{i}", [128, 8, 128], BF16) for i in range(4)]
            vt = [self.sb(st, f"vt{i}", [128, 2048], BF16) for i in range(4)]
            ktok = [self.sb(st, f"ktok{i}", [128, 1024], BF16) for i in range(2)]
            attm = [self.sb(st, f"attm{i}", [128, 128], BF16) for i in range(2)]
            ost = [self.sb(st, f"ost{i}", [128, 2048], F32) for i in range(4)]
            pat = [self.ps(st, f"pat{i}", [128, 512], F32) for i in range(2)]
            po = [self.ps(st, f"po{i}", [128, 512], F32) for i in range(2)]
            pk = [self.ps(st, f"pk{i}", [128, 512], F32) for i in range(2)]
            pt = self.ps(st, "ptk", [128, 1024], BF16)
            order = [[32, 33] + list(range(32)), [33, 32] + list(range(31, -1, -1))]
            ia = 0
            io = 0
            ik = 0
            for s_ in range(NT):
                for dr in range(2):
                    t = order[dr][s_]
                    bi = dr * 2 + s_ % 2
                    kb = (s_ * 2 + dr) % 2
                    ts = slice(t * 128, (t + 1) * 128)
                    self.DMA("sp", qt[bi][:], self.QT[dr, :, ts].rearrange("(c p) t -> p c t", p=128),
                             [], [("qt", bi)], ("qt", bi))
                    self.DMA("sp", kt[bi][:], self.KT[dr, :, ts].rearrange("(c p) t -> p c t", p=128),
                             [], [("kt", bi)], ("kt", bi))
                    self.DMA("sp", vt[bi][:], self.Vd[ts, :], [], [("vt", bi)], ("vt", bi))
                    for c in range(8):
                        self.TR(pt[:, c * 128:(c + 1) * 128], kt[bi][:, c, :], [("kt", bi), "ident"], ["ptk"])
                    self.CP("act", ktok[kb][:], pt[:], ["ptk"], [("ktok", kb)])
                    for h in range(4):
                        gh = h if dr == 0 else 3 - h
                        ch = float((1.0 - 2.0 ** (-5 - gh)) ** 128)
                        ab = ia % 2
                        ia += 1
                        ob = io % 2
                        io += 1
                        for cc in range(2):
                            self.MM(pat[ab][:, :128], kt[bi][:, 2 * h + cc, :], qt[bi][:, 2 * h + cc, :],
                                    cc == 0, cc == 1, [("kt", bi), ("qt", bi)], [("pat", ab)])
                        self.TT("dve", attm[ab][:], pat[ab][:, :128], msk[:, dr, :], ALU.mult,
                                [("pat", ab), "msk"], [("attm", ab)])
                        vh = vt[bi][:, h * 512:(h + 1) * 512]
                        self.MM(po[ob][:], attm[ab][:], vh, True, False, [("attm", ab), ("vt", bi)], [("po", ob)])
                        for cc in range(2):
                            si = (dr * 4 + h) * 2 + cc
                            self.MM(po[ob][:], qt[bi][:, 2 * h + cc, :], Sb[:, si, :], False, cc == 1,
                                    [("qt", bi), ("Sb", si)], [("po", ob)])
                        self.CP("act", ost[bi][:, h * 512:(h + 1) * 512], po[ob][:], [("po", ob)], [("ost", bi)])
                        for cc in range(2):
                            si = (dr * 4 + h) * 2 + cc
                            kb2 = ik % 2
                            ik += 1
                            self.MM(pk[kb2][:], ktok[kb][:, (2 * h + cc) * 128:(2 * h + cc + 1) * 128], vh, True, True,
                                    [("ktok", kb), ("vt", bi)], [("pk", kb2)])
                            self.STT(T[:, si, :], T[:, si, :], ch, pk[kb2][:], ALU.mult, ALU.add,
                                     [("T", si), ("pk", kb2)], [("T", si)])
                            self.S.add("act", lambda e, si=si, ch=ch: e.mul(out=Sb[:, si, :], in_=T[:, si, :], mul=ch), [("T", si)], [("Sb", si)])
                    self.DMA("pool", self.Od[dr, ts, :], ost[bi][:], [("ost", bi)], [("Od", dr, t)], ("ost", bi))
            S.flush()
        with ExitStack() as st:
            bc = self.load_bc_epi(st, L, 1, 1.0)
            wo = self.sb(st, "rwo", [128, 16, D], BF16)
            for i in range(4):
                self.DMA("pool", wo[:, i * 4:(i + 1) * 4, :],
                         w_out[i * 512:(i + 1) * 512, :].rearrange("(f p) d -> p f d", p=128),
                         [], [("wo", i)], ("wo", i))
            of = [self.sb(st, f"of{i}", [128, 2048], F32) for i in range(2)]
            obk = [self.sb(st, f"obk{i}", [128, 2048], F32) for i in range(2)]
            sgl = [self.sb(st, f"sgl{i}", [128, 2048], F32) for i in range(2)]
            obf = [self.sb(st, f"obf{i}", [128, 2048], BF16) for i in range(2)]
            oT = [self.sb(st, f"oT{i}", [128, 16, 128], BF16) for i in range(2)]
            gst6 = [self.sb(st, f"gst6{i}", [128, 4, 6], F32) for i in range(2)]
            gmv = [self.sb(st, f"gmv{i}", [128, 4, 2], F32) for i in range(2)]
            grs = [self.sb(st, f"grs{i}", [128, 4], F32) for i in range(2)]
            eb = self.make_ebufs(st)
            ptr = self.ps(st, "ptro", [128, 2048], BF16)
            py = [self.ps(st, f"py{i}", [128, D], F32) for i in range(2)]
            for t in tiles:
                par = t % 2
                ts = slice(t * 128, (t + 1) * 128)
                self.DMA("sp", of[par][:], self.Od[0, ts, :], [], [("of", par)], ("of", par))
                self.DMA("sp", obk[par][:], self.Od[1, ts, :], [], [("obk", par)], ("obk", par))
                self.DMA("sp", sgl[par][:], self.SGd[ts, :], [], [("sgl", par)], ("sgl", par))
                self.TT("pool", of[par][:], of[par][:], obk[par][:], ALU.add, [("of", par), ("obk", par)], [("of", par)])
                for h in range(4):
                    self.S.add("dve", lambda e, h=h, par=par: e.bn_stats(out=gst6[par][:, h, :],
                                                                        in_=of[par][:, h * 512:(h + 1) * 512]),
                               [("of", par)], [("gst6", par)])
                for h in range(4):
                    self.S.add("dve", lambda e, h=h, par=par: e.bn_aggr(out=gmv[par][:, h, :], in_=gst6[par][:, h, :]),
                               [("gst6", par)], [("gmv", par)])
                self.TS("dve", grs[par][:], gmv[par][:, :, 1], EPS, None, ALU.add, None, [("gmv", par)], [("grs", par)])
                self.ACT(grs[par][:], grs[par][:], AF.Sqrt, [("grs", par)], [("grs", par)])
                self.S.add("dve", lambda e, par=par: e.reciprocal(out=grs[par][:], in_=grs[par][:]),
                           [("grs", par)], [("grs", par)])
                for h in range(4):
                    hs = slice(h * 512, (h + 1) * 512)
                    self.TS("dve", of[par][:, hs], of[par][:, hs], gmv[par][:, h, 0:1], grs[par][:, h:h + 1],
                            ALU.subtract, ALU.mult, [("of", par), ("gmv", par), ("grs", par)], [("of", par)])
                self.TT("pool", obf[par][:], of[par][:], sgl[par][:], ALU.mult, [("of", par), ("sgl", par)], [("obf", par)])
                for f in range(16):
                    self.TR(ptr[:, f * 128:(f + 1) * 128], obf[par][:, f * 128:(f + 1) * 128],
                            [("obf", par), "ident"], ["ptro"])
                self.CP("act", oT[par][:].rearrange("p f t -> p (f t)"), ptr[:], ["ptro"], [("oT", par)])
                for f in range(16):
                    for hh in range(2):
                        self.MM(py[par][:, hh * 512:(hh + 1) * 512], oT[par][:, f, :], wo[:, f, hh * 512:(hh + 1) * 512],
                                f == 0, f == 15, [("oT", par), ("wo", f // 4)], [("py", par)])
                self.epilogue(t, py[par][:], [("py", par)], bc, eb)
            S.flush()

    def na_sublayer(self, L, idx, tiles):
        S = self.S
        w_qkv = self.din["na_w_qkv"][idx]
        w_out = self.din["na_w_out"][idx]
        rp = self.din["rpb_pad"]
        blks = [(i * 512, 512) for i in range(8)] + [(NLAT, NCTX)]
        alltiles = list(range(NT))
        with ExitStack() as ost_:
          OT = self.sb(ost_, "OT", [128, 8, NTOK], BF16)
          with ExitStack() as ust:
            uT = self.sb(ust, "uT", [128, 8, NTOK], BF16)
            with ExitStack() as st:
                bc = self.load_bc_prep(st, L, 1)
                self.prep(st, bc, uT, alltiles)
                S.flush()
            with ExitStack() as st:
                qTc = self.sb(st, "qTc", [128, NTOK], BF16)
                kTc = self.sb(st, "kTc", [128, NTOK], BF16)
                Ve = self.sb(st, "Ve", [128, NT, 2, 65], BF16)
                Vo = self.sb(st, "Vo", [128, 31, 2, 65], BF16)
                BT = self.sb(st, "BT", [128, 2, 15, 64], F32)
                BTr = self.sb(st, "BTr", [128, 2, 15, 64], F32)
                J2 = self.sb(st, "J2", [128, 128], F32)
                M = self.sb(st, "nam", [128, 64], F32)
                self.DMA("sp", J2[:], self.din["k_j2"][:, :], [], ["J2"], "J2")
                wq = [self.sb(st, f"nwq{i}", [128, 8, 128], BF16) for i in range(2)]
                wk = [self.sb(st, f"nwk{i}", [128, 8, 128], BF16) for i in range(2)]
                wv = [self.sb(st, f"nwv{i}", [128, 8, 128], BF16) for i in range(2)]
                sbs = [self.sb(st, f"sbs{i}", [128, 256], F32) for i in range(2)]
                pT = [self.sb(st, f"pT{i}", [128, 384], BF16) for i in range(2)]
                rec = [self.sb(st, f"rec{i}", [128, 2], F32) for i in range(2)]
                otok = [self.sb(st, f"otok{i}", [128, 2, 64], BF16) for i in range(2)]
                pp = [self.ps(st, f"npp{i}", [128, 512], F32) for i in range(2)]
                pss = [self.ps(st, f"pss{i}", [128, 512], F32) for i in range(2)]
                ppo = [self.ps(st, f"ppo{i}", [128, 2, 65], F32) for i in range(2)]
                ptr = [self.ps(st, f"nptr{i}", [128, 128], BF16) for i in range(2)]
                self.DMA("sp", M[:], self.din["k_namask"][:, :], [], ["nam"], "nam")
                self.S.add("dve", lambda e: e.memset(Ve[:], 1.0), [], ["Ve"])
                self.S.add("dve", lambda e: e.memset(Vo[:], 1.0), [], ["Vo"])
                self.S.add("dve", lambda e: e.memset(BTr[:], 0.0), [], ["BTr"])

                def load_w(c):
                    b = c % 2
                    for j, (w, nm) in enumerate(((wq, "nwq"), (wk, "nwk"), (wv, "nwv"))):
                        c0 = j * 1024 + c * 128
                        self.DMA("pool", w[b][:], w_qkv[:, c0:c0 + 128].rearrange("(k p) n -> p k n", p=128),
                                 [], [(nm, b)], (nm, b))

                cnt = {"pp": 0, "ss": 0, "po": 0, "tr": 0}

                def attend(c, q0, nq, chunks):
                    po_i = cnt["po"] % 2
                    cnt["po"] += 1
                    nloc = sum(1 for ch in chunks if ch[3] is not None)
                    nch = len(chunks)
                    for hh in range(2):
                        hp = slice(hh * 64, (hh + 1) * 64)
                        si = cnt["ss"] % 2
                        cnt["ss"] += 1
                        for m, (k0, vb, vt_, rr) in enumerate(chunks):
                            self.MM(pss[si][:, m * nq:(m + 1) * nq], kTc[hp, k0:k0 + 128], qTc[hp, q0:q0 + nq],
                                    True, True, ["kTc", "qTc"], [("pss", si)])
                        if nloc:
                            rr0 = chunks[0][3]
                            self.STT(sbs[si][:, :nloc * nq].rearrange("p (m q) -> p m q", q=nq),
                                     pss[si][:, :nloc * nq].rearrange("p (m q) -> p m q", q=nq), 0.125,
                                     BT[:, hh, rr0:rr0 + 2 * nloc:2, :], ALU.mult, ALU.add,
                                     [("pss", si), "BT"], [("sbs", si)])
                            self.ACT(pT[si][:, :nloc * nq], sbs[si][:, :nloc * nq], AF.Exp, [("sbs", si)], [("pT", si)])
                        self.ACT(pT[si][:, nloc * nq:nch * nq], pss[si][:, nloc * nq:nch * nq], AF.Exp,
                                 [("pss", si)], [("pT", si)], scale=0.125)
                        for m, (k0, vb, vt_, rr) in enumerate(chunks):
                            self.MM(ppo[po_i][:nq, hh, :], pT[si][:, m * nq:(m + 1) * nq], vb[:, vt_, hh, :],
                                    m == 0, m == nch - 1, [("pT", si), "Ve", "Vo"], [("ppo", po_i)])
                    self.S.add("dve", lambda e: e.reciprocal(out=rec[po_i][:nq, :], in_=ppo[po_i][:nq, :, 64]),
                               [("ppo", po_i)], [("rec", po_i)])
                    self.TT("dve", otok[po_i][:nq, :, :], ppo[po_i][:nq, :, 0:64],
                            rec[po_i][:nq, :].unsqueeze(2).to_broadcast([nq, 2, 64]), ALU.mult,
                            [("ppo", po_i), ("rec", po_i)], [("otok", po_i)])
                    ti = cnt["tr"] % 2
                    cnt["tr"] += 1
                    self.TR(ptr[ti][:, :nq], otok[po_i][:nq, :, :].rearrange("q a b -> q (a b)"),
                            [("otok", po_i), "ident"], [("nptr", ti)])
                    self.CP("act", OT[:, c, q0:q0 + nq], ptr[ti][:, :nq], [("nptr", ti)], [("OT", c, q0)])

                load_w(0)
                for c in range(8):
                    b = c % 2
                    if c + 1 < 8:
                        load_w(c + 1)
                    for par in range(2):
                        for hh in range(2):
                            src = bass.AP(tensor=rp.tensor, offset=((2 * c + hh) * 15 + par) * 127,
                                          ap=[[1, 64], [127, 15 - par], [1, 64]])
                            self.DMA("sp", BTr[par * 64:(par + 1) * 64, hh, 0:15 - par, :], src, [], ["BTr"],
                                     ("BTr", par, hh))
                    btr2 = BTr[:].rearrange("p a r q -> p (a r q)")
                    bt2 = BT[:].rearrange("p a r q -> p (a r q)")
                    for (c0_, c1_) in ((0, 512), (512, 1024), (1024, 1536), (1536, 1920)):
                        pi = cnt["pp"] % 2
                        cnt["pp"] += 1
                        nn = c1_ - c0_
                        self.MM(pp[pi][:, :nn], J2[:], btr2[:, c0_:c1_], True, True, ["BTr", "J2"], [("npp", pi)])
                        self.TT("dve", bt2[:, c0_:c1_].rearrange("p (a q) -> p a q", q=64),
                                pp[pi][:, :nn].rearrange("p (a q) -> p a q", q=64),
                                M[:, :].unsqueeze(1).to_broadcast([128, nn // 64, 64]), ALU.add,
                                [("npp", pi), "nam"], ["BT"])
                    for bi, (t0, n) in enumerate(blks):
                        ukeys = [("uT", t) for t in range(t0 // 128, (t0 + n) // 128)]
                        for w, nm, dstT in ((wq, "nwq", qTc), (wk, "nwk", kTc)):
                            pi = cnt["pp"] % 2
                            cnt["pp"] += 1
                            for k in range(8):
                                self.MM(pp[pi][:, :n], w[b][:, k, :], uT[:, k, t0:t0 + n], k == 0, k == 7,
                                        [(nm, b)] + ukeys, [("npp", pi)])
                            self.CP("act", dstT[:, t0:t0 + n], pp[pi][:, :n], [("npp", pi)],
                                    ["qTc" if dstT is qTc else "kTc"])
                    for vb, ntile, toff, nm in ((Ve, NT, 0, "Ve"), (Vo, 31, 64, "Vo")):
                        for g0 in range(0, ntile, 4):
                            g1 = min(ntile, g0 + 4)
                            pi = cnt["pp"] % 2
                            cnt["pp"] += 1
                            for a in range(g0, g1):
                                tk = toff + a * 128
                                for k in range(8):
                                    self.MM(pp[pi][:, (a - g0) * 128:(a - g0 + 1) * 128], uT[:, k, tk:tk + 128],
                                            wv[b][:, k, :], k == 0, k == 7,
                                            [("nwv", b), ("uT", tk // 128), ("uT", min(NT - 1, (tk + 127) // 128))],
                                            [("npp", pi)])
                            ng = g1 - g0
                            self.CP("act", vb[:, g0:g1, :, 0:64],
                                    pp[pi][:, :ng * 128].rearrange("p (a h d) -> p a h d", h=2, d=64),
                                    [("npp", pi)], [nm])
                    for r in range(64):
                        rs = min(max(r - 4, 0), 56)
                        off = rs - r + 7
                        chunks = []
                        for m in range(4):
                            row = rs + 2 * m
                            if row % 2 == 0:
                                chunks.append((row * 64, Ve, row // 2, off + 2 * m))
                            else:
                                chunks.append((row * 64, Vo, (row - 1) // 2, off + 2 * m))
                        chunks.append((NLAT, Ve, 32, None))
                        chunks.append((NLAT + 128, Ve, 33, None))
                        attend(c, r * 64, 64, chunks)
                    if len(tiles) > 32:
                        for qg in range(2):
                            attend(c, NLAT + qg * 128, 128, [(NLAT, Ve, 32, None), (NLAT + 128, Ve, 33, None)])
                S.flush()
          with ExitStack() as st:
            bc = self.load_bc_epi(st, L, 1, 1.0)
            wo = self.sb(st, "nwo", [128, 8, D], BF16)
            for i in range(2):
                self.DMA("pool", wo[:, i * 4:(i + 1) * 4, :],
                         w_out[i * 512:(i + 1) * 512, :].rearrange("(f p) d -> p f d", p=128),
                         [], [("wo", i)], ("wo", i))
            eb = self.make_ebufs(st)
            py = [self.ps(st, f"py{i}", [128, D], F32) for i in range(2)]
            for t in tiles:
                par = t % 2
                for f in range(8):
                    for hh in range(2):
                        self.MM(py[par][:, hh * 512:(hh + 1) * 512], OT[:, f, t * 128:(t + 1) * 128],
                                wo[:, f, hh * 512:(hh + 1) * 512], f == 0, f == 7, [("wo", f // 4)], [("py", par)])
                self.epilogue(t, py[par][:], [("py", par)], bc, eb)
            S.flush()

    def lru_sublayer(self, L, idx, tiles):
        S = self.S
        w_in = self.din["lru_w_in"][idx]
        w_out = self.din["lru_w_out"][idx]
        blks = [(i * 512, 512) for i in range(8)] + [(NLAT, NCTX)]
        alltiles = list(range(NT))
        LAT0, CTX0, XW = 2, NLAT + 5, NTOK + 6
        with ExitStack() as st:
            bc = self.load_bc_prep(st, L, 1)
            uT = self.sb(st, "uT", [128, 8, NTOK], BF16)
            ensure = self.block_prepper(self.prep_alloc(st, bc, uT), blks)
            wx = [self.sb(st, f"lwx{i}", [128, 8, 128], BF16) for i in range(2)]
            wg = [self.sb(st, f"lwg{i}", [128, 8, 128], BF16) for i in range(2)]
            xs = [self.sb(st, f"lxs{i}", [128, 512], F32) for i in range(2)]
            gs = [self.sb(st, f"lgs{i}", [128, 512], F32) for i in range(2)]
            px = [self.ps(st, f"lpx{i}", [128, 512], F32) for i in range(2)]
            pg = [self.ps(st, f"lpg{i}", [128, 512], F32) for i in range(2)]

            def load_w(c):
                b = c % 2
                self.DMA("pool", wg[b][:], w_in[:, c * 128:(c + 1) * 128].rearrange("(k p) n -> p k n", p=128),
                         [], [("lwg", b)], ("lwg", b))
                self.DMA("pool", wx[b][:], w_in[:, D + c * 128:D + (c + 1) * 128].rearrange("(k p) n -> p k n", p=128),
                         [], [("lwx", b)], ("lwx", b))

            it = 0
            load_w(0)
            for c in range(8):
                b = c % 2
                if c + 1 < 8:
                    load_w(c + 1)
                for bi, (t0, n) in enumerate(blks):
                    if c == 0:
                        ensure(bi + 2)
                    pb = it % 2
                    it += 1
                    ukeys = [("uT", t) for t in range(t0 // 128, (t0 + n) // 128)]
                    for k in range(8):
                        self.MM(px[pb][:, :n], wx[b][:, k, :], uT[:, k, t0:t0 + n], k == 0, k == 7,
                                [("lwx", b)] + ukeys, [("lpx", pb)])
                    for k in range(8):
                        self.MM(pg[pb][:, :n], wg[b][:, k, :], uT[:, k, t0:t0 + n], k == 0, k == 7,
                                [("lwg", b)] + ukeys, [("lpg", pb)])
                    self.CP("dve", xs[pb][:, :n], px[pb][:, :n], [("lpx", pb)], [("lxs", pb)])
                    self.ACT(gs[pb][:, :n], pg[pb][:, :n], AF.Gelu, [("lpg", pb)], [("lgs", pb)])
                    self.DMA("pool", self.XR[c * 128:(c + 1) * 128, t0:t0 + n], xs[pb][:, :n], [("lxs", pb)],
                             [("XR", c, bi)], ("lxs", pb))
                    self.DMA("pool", self.GG[c * 128:(c + 1) * 128, t0:t0 + n], gs[pb][:, :n], [("lgs", pb)],
                             [("GG", c, bi)], ("lgs", pb))
            S.flush()
        with ExitStack() as st:
            cols = self.sb(st, "lcols", [128, 8, 11], F32)
            self.DMA("sp", cols[:], self.din["lru_cols"][:, :, :], [], ["lcols"], "lcols")
            sp8 = self.sb(st, "lsp8", [128, 8, 2], F32)
            tA = self.sb(st, "ltA", [128, 8, 2], F32)
            tB = self.sb(st, "ltB", [128, 8, 2], F32)
            lam = cols[:, :, 9:11]
            self.TS("dve", tB[:], lam, -1.0, None, ALU.mult, None, ["lcols"], ["ltB"])
            self.TT("dve", tA[:], lam, tB[:], ALU.max, ["lcols", "ltB"], ["ltA"])
            self.ACT(tA[:], tA[:], AF.Exp, ["ltA"], ["ltA"], scale=-1.0)
            self.ACT(tA[:], tA[:], AF.Ln, ["ltA"], ["ltA"], bias=1.0)
            self.TS("dve", tB[:], tB[:], 0.0, None, ALU.max, None, ["ltB", "ltA"], ["ltB"])
            self.TT("dve", sp8[:], tA[:], tB[:], ALU.add, ["ltA", "ltB"], ["lsp8"])
            self.TS("dve", sp8[:], sp8[:], -8.0, None, ALU.mult, None, ["lsp8"], ["lsp8"])
            xp = self.sb(st, "lxp", [128, 2, XW], F32)
            xc = self.sb(st, "lxc", [128, 2, NTOK], F32)
            xcb = self.sb(st, "lxcb", [128, 2, NTOK], BF16)
            A = self.sb(st, "lA", [128, NTOK], F32)
            G = self.sb(st, "lG", [128, NTOK], F32)
            Tm = self.sb(st, "lTm", [128, NTOK], F32)
            HS = self.sb(st, "lHS", [128, NTOK], F32)
            gg = self.sb(st, "lgg", [128, NTOK], F32)
            yb = self.sb(st, "lyb", [128, NTOK], BF16)
            wa = [self.sb(st, f"lwa{i}", [128, 2, 256], BF16) for i in range(2)]
            wxg = [self.sb(st, f"lwxg{i}", [128, 2, 256], BF16) for i in range(2)]
            pr = [self.ps(st, f"lpr{i}", [128, 512], F32) for i in range(2)]
            pi_ = [self.ps(st, f"lpi{i}", [128, 512], F32) for i in range(2)]
            self.S.add("dve", lambda e: e.memset(xp[:], 0.0), [], ["lxp"])
            it = 0
            wi = 0
            for kb in range(4):
                for cc in range(2):
                    c = 2 * kb + cc
                    self.DMA("sp", xp[:, cc, LAT0:LAT0 + NLAT], self.XR[c * 128:(c + 1) * 128, 0:NLAT],
                             [], ["lxp"], ("lxp", cc))
                    self.DMA("sp", xp[:, cc, CTX0:CTX0 + NCTX], self.XR[c * 128:(c + 1) * 128, NLAT:NTOK],
                             [], ["lxp"], ("lxp", cc))
                    for (o0, x0, n) in ((0, LAT0 - 2, NLAT), (NLAT, CTX0 - 2, NCTX)):
                        dsto = xc[:, cc, o0:o0 + n]
                        self.TS("dve", dsto, xp[:, cc, x0:x0 + n], cols[:, c, 0:1], cols[:, c, 4:5],
                                ALU.mult, ALU.add, ["lxp", "lcols"], [("lxc", cc)])
                        for j in range(1, 4):
                            self.STT(dsto, xp[:, cc, x0 + j:x0 + j + n], cols[:, c, j:j + 1], dsto, ALU.mult, ALU.add,
                                     ["lxp", "lcols", ("lxc", cc)], [("lxc", cc)])
                    self.CP("pool", xcb[:, cc, :], xc[:, cc, :], [("lxc", cc)], [("lxcb", cc)])
                for cc in range(2):
                    c = 2 * kb + cc
                    self.DMA("sp", gg[:], self.GG[c * 128:(c + 1) * 128, :], [], ["lgg"], "lgg")
                    for d in range(2):
                        wb = wi % 2
                        wi += 1
                        self.DMA("pool", wa[wb][:],
                                 self.din["lru_w_a"][idx, d, kb].rearrange("(ic p) j -> p ic j", p=128),
                                 [], [("lwa", wb)], ("lwa", wb))
                        self.DMA("pool", wxg[wb][:],
                                 self.din["lru_w_x"][idx, d, kb].rearrange("(ic p) j -> p ic j", p=128),
                                 [], [("lwxg", wb)], ("lwxg", wb))
                        for bi, (t0, n) in enumerate(blks):
                            pb = it % 2
                            it += 1
                            for ic in range(2):
                                self.MM(pr[pb][:, :n], wa[wb][:, ic, cc * 128:(cc + 1) * 128], xcb[:, ic, t0:t0 + n],
                                        ic == 0, ic == 1, [("lwa", wb), ("lxcb", 0), ("lxcb", 1)], [("lpr", pb)])
                            for ic in range(2):
                                self.MM(pi_[pb][:, :n], wxg[wb][:, ic, cc * 128:(cc + 1) * 128], xcb[:, ic, t0:t0 + n],
                                        ic == 0, ic == 1, [("lwxg", wb), ("lxcb", 0), ("lxcb", 1)], [("lpi", pb)])
                            self.ACT(A[:, t0:t0 + n], pr[pb][:, :n], AF.Sigmoid, [("lpr", pb), "lcols"], ["lA"],
                                     bias=cols[:, c, 5 + d:6 + d])
                            self.ACT(G[:, t0:t0 + n], pi_[pb][:, :n], AF.Sigmoid, [("lpi", pb), "lcols"], ["lG"],
                                     bias=cols[:, c, 7 + d:8 + d])
                        self.ACT(A[:], A[:], AF.Exp, ["lA", "lsp8"], ["lA"], scale=sp8[:, c, d:d + 1])
                        self.TT("pool", Tm[:], A[:], A[:], ALU.mult, ["lA"], ["lTm"])
                        self.ACT(Tm[:], Tm[:], AF.Sqrt, ["lTm"], ["lTm"], bias=1.0, scale=-1.0)
                        self.TT("dve", G[:], G[:], xc[:, cc, :], ALU.mult, ["lG", ("lxc", cc)], ["lG"])
                        self.TT("dve", G[:], G[:], Tm[:], ALU.mult, ["lG", "lTm"], ["lG"])
                        dst = HS if d == 0 else Tm
                        dkey = "lHS" if d == 0 else "lTm"
                        if d == 0:
                            self.S.add("dve", lambda e, dst=dst: e.tensor_tensor_scan(
                                out=dst[:, NLAT:NTOK], data0=A[:, NLAT:NTOK], data1=G[:, NLAT:NTOK], initial=0.0,
                                op0=ALU.mult, op1=ALU.add), ["lA", "lG"], [dkey])
                            self.S.add("dve", lambda e, dst=dst: e.tensor_tensor_scan(
                                out=dst[:, 0:NLAT], data0=A[:, 0:NLAT], data1=G[:, 0:NLAT],
                                initial=dst[:, NTOK - 1:NTOK], op0=ALU.mult, op1=ALU.add), ["lA", "lG", dkey], [dkey])
                        else:
                            self.S.add("dve", lambda e, dst=dst: e.tensor_tensor_scan(
                                out=dst[:, NLAT:NTOK][:, ::-1], data0=A[:, NLAT:NTOK][:, ::-1],
                                data1=G[:, NLAT:NTOK][:, ::-1], initial=0.0,
                                op0=ALU.mult, op1=ALU.add), ["lA", "lG"], [dkey])
                            self.S.add("dve", lambda e, dst=dst: e.tensor_tensor_scan(
                                out=dst[:, 0:NLAT][:, ::-1], data0=A[:, 0:NLAT][:, ::-1], data1=G[:, 0:NLAT][:, ::-1],
                                initial=dst[:, NLAT:NLAT + 1], op0=ALU.mult, op1=ALU.add), ["lA", "lG", dkey], [dkey])
                            self.TT("pool", HS[:], HS[:], Tm[:], ALU.add, ["lHS", "lTm"], ["lHS"])
                    self.TT("dve", yb[:], gg[:], HS[:], ALU.mult, ["lgg", "lHS"], ["lyb"])
                    self.DMA("pool", self.YT[c * 128:(c + 1) * 128, :], yb[:], ["lyb"], [("YT", c)], "lyb")
            S.flush()
        with ExitStack() as st:
            bc = self.load_bc_epi(st, L, 1, 1.0)
            wo = self.sb(st, "lwo", [128, 8, D], BF16)
            for i in range(2):
                self.DMA("pool", wo[:, i * 4:(i + 1) * 4, :],
                         w_out[i * 512:(i + 1) * 512, :].rearrange("(f p) d -> p f d", p=128),
                         [], [("wo", i)], ("wo", i))
            ybk = [self.sb(st, f"lybk{i}", [128, 8, 512], BF16) for i in range(2)]
            eb = self.make_ebufs(st)
            py = [self.ps(st, f"py{i}", [128, D], F32) for i in range(2)]
            ublks = blks if len(tiles) > 32 else blks[:8]
            for bi, (t0, n) in enumerate(ublks):
                b = bi % 2
                self.DMA("sp", ybk[b][:, :, :n], self.YT[:, t0:t0 + n].rearrange("(f p) t -> p f t", p=128),
                         [], [("lybk", b)], ("lybk", b))
                for ti in range(n // 128):
                    t = t0 // 128 + ti
                    par = t % 2
                    for f in range(8):
                        for hh in range(2):
                            self.MM(py[par][:, hh * 512:(hh + 1) * 512], ybk[b][:, f, ti * 128:(ti + 1) * 128],
                                    wo[:, f, hh * 512:(hh + 1) * 512], f == 0, f == 7,
                                    [("lybk", b), ("wo", f // 4)], [("py", par)])
                    self.epilogue(t, py[par][:], [("py", par)], bc, eb)
            S.flush()


_WEIGHTS = ("ada_w", "ada_b", "ln_g", "ln_b", "ffn_w_in", "ffn_w_out", "ret_w_in", "ret_w_out",
            "na_w_qkv", "na_rpb", "na_w_out", "lru_w_in", "lru_conv_w", "lru_conv_b", "lru_w_a",
            "lru_b_a", "lru_w_x", "lru_b_x", "lru_lam", "lru_w_out")


def const_inputs():
    p = np.arange(128)
    freqs = 10000.0 ** (-(2.0 * (p % 64)) / 128.0)
    ang = np.arange(64, dtype=np.float64)[None, :] * freqs[:, None]
    cos = np.cos(ang)
    sin = np.sin(ang) * np.where(p < 64, -1.0, 1.0)[:, None]
    dec = np.zeros((16, 128), np.float64)
    i = np.arange(128, dtype=np.float64)
    for dr in range(2):
        for h in range(4):
            gh = h if dr == 0 else 3 - h
            lg = math.log1p(-(2.0 ** (-5 - gh)))
            e = (i + 1.0) if dr == 0 else (128.0 - i)
            dec[dr * 8 + h] = np.exp(lg * e)
            dec[dr * 8 + 4 + h] = np.exp(-lg * e) / 16.0
    j = np.arange(128)[:, None]
    ii = np.arange(128)[None, :]
    mask = np.stack([(j <= ii), (j > ii)], axis=1).astype(np.float32)
    kc = np.arange(64)[:, None]
    qc = np.arange(64)[None, :]
    cs = np.clip(qc - 8, 0, 48)
    ok = (kc >= cs) & (kc < cs + 16)
    namask = np.where(np.concatenate([ok, ok], axis=0), 0.0, -1e30).astype(np.float32)
    j2 = np.zeros((128, 128), np.float32)
    for par in range(2):
        for a in range(64):
            j2[par * 64 + a, par * 64 + 63 - a] = 1.0
    return {"k_ident": np.eye(128, dtype=np.float32), "k_namask": namask, "k_j2": j2,
            "k_cos": cos.astype(np.float32), "k_sin": sin.astype(np.float32),
            "k_dec": np.ascontiguousarray(np.broadcast_to(dec.astype(np.float32)[None], (128, 16, 128))),
            "k_mask": np.ascontiguousarray(mask)}


def make_in_map(inputs, b, consts):
    m = {k: np.ascontiguousarray(inputs[k], dtype=np.float32) for k in _WEIGHTS}
    m["x"] = np.ascontiguousarray(inputs["x"][b], dtype=np.float32)
    m["ctx"] = np.ascontiguousarray(inputs["ctx"][b], dtype=np.float32)
    cl = np.asarray(inputs["c"][b], dtype=np.float32).reshape(8, 128).T
    cc = np.asarray(inputs["c_ctx"], dtype=np.float32).reshape(8, 128).T
    m["c2"] = np.ascontiguousarray(np.stack([cl, cc], axis=-1))
    colv = [np.asarray(inputs["lru_conv_w"], np.float32)[0, j] for j in range(4)]
    colv.append(np.asarray(inputs["lru_conv_b"], np.float32)[0])
    for nm in ("lru_b_a", "lru_b_x", "lru_lam"):
        colv += [np.asarray(inputs[nm], np.float32)[0, 0], np.asarray(inputs[nm], np.float32)[0, 1]]
    m["lru_cols"] = np.ascontiguousarray(np.stack(colv, 0).reshape(11, 8, 128).transpose(2, 1, 0))
    rpb = np.asarray(inputs["na_rpb"], dtype=np.float32)[0]
    rp = np.zeros((16, 15, 127), np.float32)
    rp[:, :, 48:79] = rpb[:, :, ::-1]
    m["rpb_pad"] = rp
    m.update(consts)
    return m


def kernel(**inputs):
    prog = Prog()
    nc = prog.build()
    consts = const_inputs()
    in_maps = [make_in_map(inputs, b, consts) for b in range(N_CORES)]
    res = run_bass_kernel_spmd(nc, in_maps, core_ids=list(range(N_CORES)))
    return np.stack([np.asarray(r["out"], dtype=np.float32) for r in res.results], axis=0)
```
